# Optimizing a Trainium2 kernel written in Bass

```python
import jax, jax.numpy as jnp
from jax import lax
import numpy as np

D_MODEL = 1024
BATCH = 4
SEQ = 4096
DEPTH = 1

N_MEM = 256
HEAD_DIM = 64
ATTN_WIDTH = D_MODEL // 2
CONV_WIDTH = D_MODEL // 4
XATTN_WIDTH = D_MODEL // 4
N_ATTN_HEADS = ATTN_WIDTH // HEAD_DIM
N_XATTN_HEADS = 4
XATTN_HEAD_DIM = XATTN_WIDTH // N_XATTN_HEADS
MIX_WIDTH = ATTN_WIDTH + CONV_WIDTH + XATTN_WIDTH
IN_PROJ_WIDTH = 3 * ATTN_WIDTH + 3 * CONV_WIDTH + XATTN_WIDTH
DILATED_PATTERNS = ((128, 1), (512, 4), (2048, 16))
CONV_K = 3
D_FF = 4 * D_MODEL
ROPE_THETA = 10000.0
EPS = 1e-6
NEG_INF = -1e30

kernel_name = "hybrid_dilated_attn_shortconv_memxattn_block"


def rms_norm(x, g):
    xf = x.astype(jnp.float32)
    y = xf * lax.rsqrt(jnp.mean(xf * xf, axis=-1, keepdims=True) + EPS)
    return (y * g.astype(jnp.float32)).astype(x.dtype)


def apply_rope(t, positions):
    dh = t.shape[-1]
    half = dh // 2
    inv_freq = jnp.float32(ROPE_THETA) ** (-(jnp.arange(half, dtype=jnp.float32) * 2.0 / dh))
    ang = positions.astype(jnp.float32)[..., None] * inv_freq
    cos = jnp.cos(ang)[:, :, None, :]
    sin = jnp.sin(ang)[:, :, None, :]
    tf = t.astype(jnp.float32)
    t1, t2 = tf[..., :half], tf[..., half:]
    out = jnp.concatenate([t1 * cos - t2 * sin, t1 * sin + t2 * cos], axis=-1)
    return out.astype(t.dtype)


def dilated_window_attention(q, k, v, window, dilation):
    B, S, H, Dh = q.shape
    L = S // dilation
    n_back = window // dilation
    blk = n_back
    nb = -(-L // blk)
    Lp = nb * blk

    def to_blocks(t):
        t = t.reshape(B, L, dilation, H, Dh).transpose(0, 2, 1, 3, 4)
        t = jnp.pad(t, ((0, 0), (0, 0), (0, Lp - L), (0, 0), (0, 0)))
        return t.reshape(B, dilation, nb, blk, H, Dh)

    qb, kb, vb = to_blocks(q), to_blocks(k), to_blocks(v)

    def with_prev(t):
        prev = jnp.pad(t, ((0, 0), (0, 0), (1, 0), (0, 0), (0, 0), (0, 0)))[:, :, :-1]
        return jnp.concatenate([prev, t], axis=3)

    kw, vw = with_prev(kb), with_prev(vb)
    scale = Dh ** -0.5
    s = jnp.einsum('bdnqhc,bdnkhc->bdnhqk', qb, kw,
                   preferred_element_type=jnp.float32) * scale
    qi = jnp.arange(blk)[:, None]
    kj = jnp.arange(2 * blk)[None, :]
    band = (kj >= qi) & (kj <= qi + n_back)
    valid = band[None] & ((jnp.arange(nb)[:, None, None] > 0) | (kj[None] >= blk))
    s = jnp.where(valid[None, None, :, None], s, NEG_INF)
    lse = jax.nn.logsumexp(s, axis=-1)
    p = jnp.exp(s - lse[..., None])
    o = jnp.einsum('bdnhqk,bdnkhc->bdnqhc', p.astype(v.dtype), vw,
                   preferred_element_type=jnp.float32)
    o = o.reshape(B, dilation, Lp, H, Dh)[:, :, :L]
    o = o.transpose(0, 2, 1, 3, 4).reshape(B, S, H, Dh)
    lse = lse.transpose(0, 1, 2, 4, 3).reshape(B, dilation, Lp, H)[:, :, :L]
    lse = lse.transpose(0, 2, 1, 3).reshape(B, S, H)
    return o, lse


def dilated_mixture_attention(q, k, v):
    outs, lses = [], []
    for window, dilation in DILATED_PATTERNS:
        o, lse = dilated_window_attention(q, k, v, window, dilation)
        outs.append(o)
        lses.append(lse)
    w = jax.nn.softmax(jnp.stack(lses, axis=0), axis=0)
    o = jnp.sum(w[..., None] * jnp.stack(outs, axis=0), axis=0)
    return o.astype(q.dtype)


def short_gated_conv(b_gate, c_gate, u, conv_w):
    z = c_gate * u
    S = z.shape[1]
    zp = jnp.pad(z, ((0, 0), (CONV_K - 1, 0), (0, 0)))
    y = zp[:, 0:S] * conv_w[0]
    for tap in range(1, CONV_K):
        y = y + zp[:, tap:tap + S] * conv_w[tap]
    return b_gate * y


def memory_cross_attention(qx, mem_kv):
    B, S, _ = qx.shape
    q = qx.reshape(B, S, N_XATTN_HEADS, XATTN_HEAD_DIM)
    km, vm = jnp.split(mem_kv, 2, axis=-1)
    km = km.reshape(B, -1, N_XATTN_HEADS, XATTN_HEAD_DIM)
    vm = vm.reshape(B, -1, N_XATTN_HEADS, XATTN_HEAD_DIM)
    s = jnp.einsum('bshc,bmhc->bhsm', q, km,
                   preferred_element_type=jnp.float32) * (XATTN_HEAD_DIM ** -0.5)
    p = jax.nn.softmax(s, axis=-1)
    o = jnp.einsum('bhsm,bmhc->bshc', p.astype(vm.dtype), vm)
    return o.reshape(B, S, XATTN_WIDTH)


def setup_inputs(seed: int = 0) -> dict:
    key = jax.random.key(seed)
    ks = jax.random.split(key, 20)
    f32 = jnp.float32

    def w(k, shape, fan_in):
        return jax.random.normal(k, shape, f32) * (fan_in ** -0.5)

    def gain(k, width):
        return 1.0 + 0.05 * jax.random.normal(k, (DEPTH, width), f32)

    x = jax.random.normal(ks[0], (BATCH, SEQ, D_MODEL), f32)
    mem = jax.random.normal(ks[1], (BATCH, N_MEM, D_MODEL), f32)
    offset = jax.random.randint(ks[2], (BATCH, 1), 0, 1024, dtype=jnp.int32)
    positions = offset + jnp.arange(SEQ, dtype=jnp.int32)[None, :]
    return {
        "x": x,
        "mem": mem,
        "positions": positions,
        "g_pre_mix": gain(ks[3], D_MODEL),
        "g_mem": gain(ks[4], D_MODEL),
        "w_in": w(ks[5], (DEPTH, D_MODEL, IN_PROJ_WIDTH), D_MODEL),
        "w_mem_kv": w(ks[6], (DEPTH, D_MODEL, 2 * XATTN_WIDTH), D_MODEL),
        "conv_w": w(ks[7], (DEPTH, CONV_K, CONV_WIDTH), CONV_K),
        "g_attn_out": gain(ks[8], ATTN_WIDTH),
        "g_conv_out": gain(ks[9], CONV_WIDTH),
        "g_xattn_out": gain(ks[10], XATTN_WIDTH),
        "w_out": w(ks[11], (DEPTH, MIX_WIDTH, D_MODEL), MIX_WIDTH),
        "g_post_mix": gain(ks[12], D_MODEL),
        "g_pre_mlp": gain(ks[13], D_MODEL),
        "w_up": w(ks[14], (DEPTH, D_MODEL, D_FF), D_MODEL),
        "w_down": w(ks[15], (DEPTH, D_FF, D_MODEL), D_FF),
        "g_post_mlp": gain(ks[16], D_MODEL),
    }


def reference(x, mem, positions, g_pre_mix, g_mem, w_in, w_mem_kv, conv_w,
              g_attn_out, g_conv_out, g_xattn_out, w_out, g_post_mix,
              g_pre_mlp, w_up, w_down, g_post_mlp):
    B, S, _ = x.shape
    a0 = ATTN_WIDTH
    c0 = 3 * ATTN_WIDTH
    x0 = 3 * ATTN_WIDTH + 3 * CONV_WIDTH
    for l in range(DEPTH):
        h = rms_norm(x, g_pre_mix[l])
        proj = jnp.einsum('bsd,de->bse', h, w_in[l])

        q = proj[..., 0:a0].reshape(B, S, N_ATTN_HEADS, HEAD_DIM)
        k = proj[..., a0:2 * a0].reshape(B, S, N_ATTN_HEADS, HEAD_DIM)
        v = proj[..., 2 * a0:3 * a0].reshape(B, S, N_ATTN_HEADS, HEAD_DIM)
        q = apply_rope(q, positions)
        k = apply_rope(k, positions)
        y_attn = dilated_mixture_attention(q, k, v).reshape(B, S, ATTN_WIDTH)

        b_gate = proj[..., c0:c0 + CONV_WIDTH]
        c_gate = proj[..., c0 + CONV_WIDTH:c0 + 2 * CONV_WIDTH]
        u = proj[..., c0 + 2 * CONV_WIDTH:c0 + 3 * CONV_WIDTH]
        y_conv = short_gated_conv(b_gate, c_gate, u, conv_w[l])

        qx = proj[..., x0:x0 + XATTN_WIDTH]
        mem_kv = jnp.einsum('bmd,de->bme', rms_norm(mem, g_mem[l]), w_mem_kv[l])
        y_x = memory_cross_attention(qx, mem_kv)

        y = jnp.concatenate([rms_norm(y_attn, g_attn_out[l]),
                             rms_norm(y_conv, g_conv_out[l]),
                             rms_norm(y_x, g_xattn_out[l])], axis=-1)
        y = jnp.einsum('bse,ed->bsd', y, w_out[l])
        x = x + rms_norm(y, g_post_mix[l])

        h2 = rms_norm(x, g_pre_mlp[l])
        f = jnp.square(jax.nn.relu(jnp.einsum('bsd,df->bsf', h2, w_up[l])))
        f = jnp.einsum('bsf,fd->bsd', f, w_down[l])
        x = x + rms_norm(f, g_post_mlp[l])
    return x
```

```python
import contextlib
import types
import numpy as np
import concourse.bass as bass
import concourse.mybir as mybir
from concourse.bass_utils import run_bass_kernel_spmd

F32 = mybir.dt.float32
BF16 = mybir.dt.bfloat16
I32 = mybir.dt.int32
AF = mybir.ActivationFunctionType
ALU = mybir.AluOpType

D = 1024
SEQ = 4096
NB = 4
NOWN = 2048
NT = 16
DFF = 4096
EPS = 1e-6
PI = float(np.pi)
TWO_PI = float(2 * np.pi)
ARENA_WORDS = 51200
REORDER = True
ALLSYNC = True


def _freeze(fn):
    if fn.__closure__ is None:
        return fn
    cells = []
    for c in fn.__closure__:
        try:
            cells.append(types.CellType(c.cell_contents))
        except ValueError:
            cells.append(c)
    return types.FunctionType(fn.__code__, fn.__globals__, fn.__name__, fn.__defaults__, tuple(cells))


class Prog:
    ENG = {"sync": "sync", "act": "scalar", "pool": "gpsimd", "pe": "tensor", "dve": "vector"}
    DEF_COST = {"sync": 0.05, "act": 0.6, "pool": 0.9, "pe": 0.25, "dve": 0.5}
    WINDOW = {"sync": 8, "act": 48, "pool": 48, "pe": 192, "dve": 48}

    def __init__(self, nc, reorder=True):
        self.nc = nc
        self.ops = []
        self.last_w = {}
        self.readers = {}
        self.phase = 0
        self.noreorder = set()
        self.reorder = reorder

    def barrier(self, reorder=True):
        self.phase += 1
        if not reorder:
            self.noreorder.add(self.phase)

    def _add(self, eng, fn, reads, writes, dma_stream=None, extra=(), c=None):
        deps = {}
        for d in extra:
            deps[d] = "raw"
        for k in reads:
            if k in self.last_w:
                deps[self.last_w[k]] = "raw"
        for k in writes:
            if k in self.last_w:
                deps.setdefault(self.last_w[k], "waw")
            for r in self.readers.get(k, ()):
                deps.setdefault(r, "war")
        idx = len(self.ops)
        deps.pop(idx, None)
        if c is None:
            c = self.DEF_COST[eng]
        self.ops.append(dict(eng=eng, fn=_freeze(fn), deps=deps, dma=dma_stream, phase=self.phase, c=c))
        for k in writes:
            self.last_w[k] = idx
            self.readers[k] = []
        for k in reads:
            self.readers.setdefault(k, []).append(idx)
        return idx

    def op(self, eng, fn, reads=(), writes=(), extra=(), c=None):
        return self._add(eng, fn, reads, writes, None, extra, c)

    def dma(self, eng, fn, reads=(), writes=(), stream="d0", extra=(), c=None):
        return self._add(eng, fn, reads, writes, stream, extra, 2.5 if c is None else c)

    def _schedule(self):
        ops = self.ops
        nph = self.phase + 1
        per = {e: [[] for _ in range(nph)] for e in self.ENG}
        for i, o in enumerate(ops):
            per[o["eng"]][o["phase"]].append(i)
        order = {e: [] for e in self.ENG}
        phase_tokens = []
        done = {}
        t_eng = {e: 0.0 for e in self.ENG}
        for ph in range(nph):
            queues = {e: list(per[e][ph]) for e in self.ENG}
            tstart = max(t_eng.values())
            for e in self.ENG:
                t_eng[e] = max(t_eng[e], tstart)
            remaining = sum(len(q) for q in queues.values())
            while remaining:
                best = None
                for e, q in queues.items():
                    if not q:
                        continue
                    w = self.WINDOW[e] if (self.reorder and ph not in self.noreorder) else 1
                    dma_blocked = False
                    for pos in range(min(w, len(q))):
                        i = q[pos]
                        o = ops[i]
                        if o["dma"] is not None:
                            if dma_blocked:
                                continue
                            dma_blocked = True
                        ready = tstart
                        ok = True
                        for d in o["deps"]:
                            if ops[d]["phase"] < ph:
                                continue
                            t = done.get(d)
                            if t is None:
                                ok = False
                                break
                            if t > ready:
                                ready = t
                        if not ok:
                            continue
                        st = max(ready, t_eng[e])
                        cand = (st, i, e, pos)
                        if best is None or cand < best:
                            best = cand
                        if st <= t_eng[e]:
                            break
                assert best is not None, "scheduler deadlock"
                st, i, e, pos = best
                o = ops[i]
                queues[e].pop(pos)
                order[e].append(i)
                if o["dma"] is not None:
                    t_eng[e] = st + (1.0 if e == "pool" else 0.1)
                    done[i] = st + o["c"]
                else:
                    t_eng[e] = st + o["c"]
                    done[i] = st + o["c"] + 0.1
                remaining -= 1
            toks = []
            for e in self.ENG:
                comp = [i for i in order[e] if ops[i]["phase"] == ph and ops[i]["dma"] is None]
                if comp:
                    toks.append(comp[-1])
            toks += [i for i in per_all_dma(ops, ph)]
            phase_tokens.append(toks)
            t_eng = {e: max(t_eng.values()) for e in self.ENG}
        self.sim_time = max(t_eng.values())
        return order, phase_tokens

    def emit(self, final_waits=()):
        nc = self.nc
        ops = self.ops
        order, phase_tokens = self._schedule()
        pool = {"ldw": 16}
        cnt = {}
        val = {}
        dma_prev = {}
        nstream = {}
        last_on = {}
        issue_pos = {}
        for e in self.ENG:
            for n, i in enumerate(order[e]):
                issue_pos[i] = n
        for e in self.ENG:
            for i in order[e]:
                o = ops[i]
                if o["dma"] is None:
                    continue
                n = nstream.get(o["dma"], 0)
                nstream[o["dma"]] = n + 1
                key = ("dma", "%s%d" % (o["dma"], n % pool.get(o["dma"], 4)))
                if key in last_on:
                    dma_prev[i] = last_on[key]
                last_on[key] = i
                cnt[key] = cnt.get(key, 0) + 16
                val[i] = (key, cnt[key])

        def needs_wait(i, d):
            o, p = ops[i], ops[d]
            if p["dma"] is None and o["dma"] is None and p["eng"] == o["eng"]:
                if p["eng"] == "pe":
                    return False
                return ALLSYNC or o["deps"].get(d) == "raw"
            return True

        waited = set()
        for i, o in enumerate(ops):
            for d in o["deps"]:
                if needs_wait(i, d):
                    waited.add(d)
        for toks in phase_tokens:
            waited.update(toks)
        waited.update(final_waits)
        for e in self.ENG:
            k = ("eng", e)
            for i in order[e]:
                if ops[i]["dma"] is None and i in waited:
                    cnt[k] = cnt.get(k, 0) + 1
                    val[i] = (k, cnt[k])
        keys = sorted(set(k for k, _ in val.values()))
        with contextlib.ExitStack() as es:
            sems = {}
            for k in keys:
                sems[k] = es.enter_context(nc.semaphore("s_%s_%s" % k))
            block = es.enter_context(nc.Block())

            def make(engname):
                idxs = order[engname]

                def body(e):
                    seen = {}

                    def wait(d):
                        if d not in val:
                            return
                        k, v = val[d]
                        if seen.get(k, 0) >= v:
                            return
                        seen[k] = v
                        e.wait_ge(sems[k], v)
                    for i in idxs:
                        o = ops[i]
                        for ph in range(o["phase"]):
                            for d in phase_tokens[ph]:
                                if ops[d]["eng"] != engname or ops[d]["dma"] is not None or o["dma"] is not None:
                                    wait(d)
                        for d in o["deps"]:
                            if needs_wait(i, d):
                                wait(d)
                        if i in dma_prev:
                            wait(dma_prev[i])
                        ins = o["fn"](e)
                        if i in val:
                            k, v = val[i]
                            ins.then_inc(sems[k], 16 if o["dma"] is not None else 1)
                    if engname == "sync":
                        for d in final_waits:
                            wait(d)
                return body

            for engname, attr in self.ENG.items():
                if order[engname] or engname == "sync":
                    getattr(block, attr)(make(engname))


def per_all_dma(ops, ph):
    return [i for i, o in enumerate(ops) if o["phase"] == ph and o["dma"] is not None]


def build_program(debug=(), upto=None):
    nc = bass.Bass("TRN2", target_bir_lowering=False)
    dram_in = {}

    def din(name, shape, dt=F32):
        dram_in[name] = nc.dram_tensor(name, list(shape), dt, kind="ExternalInput").ap()
        return dram_in[name]

    xo = din("xo", [NOWN, D]); xh = din("xh", [NOWN, D]); memd = din("mem", [256, D])
    posT = din("posT", [128, 32], I32); hbd = din("hb", [128, 1])
    w_in = din("w_in", [D, 2560]); w_kv = din("w_mem_kv", [D, 512]); w_out = din("w_out", [D, D])
    w_up = din("w_up", [D, DFF]); w_down = din("w_down", [DFF, D])
    gcold = din("gcol", [128, 40])
    gpmd = din("g_post_mix", [D]); gpld = din("g_post_mlp", [D])
    identd = din("ident", [128, 128]); maskd = din("mask2", [128, 256]); invfd = din("invf", [128, 32])
    Ed = din("Esel", [8, 512])
    outd = nc.dram_tensor("out", [NOWN, D], F32, kind="ExternalOutput").ap()
    dbg_out = {}

    with contextlib.ExitStack() as es:
        arena = es.enter_context(nc.sbuf_tensor("arena", [128, ARENA_WORDS], F32))
        psum = es.enter_context(nc.psum_tensor("psum", [128, 4096], F32))

        def view(off, shape, dt):
            n = int(np.prod(shape[1:]))
            nb = n * (2 if dt == BF16 else 4)
            assert nb % 4 == 0
            nw = nb // 4
            assert off + nw <= ARENA_WORDS, (off, nw)
            a = arena[0:shape[0], off:off + nw]
            if dt != F32:
                a = a.bitcast(dt)
            if len(shape) == 3:
                a = a.rearrange("p (a b) -> p a b", a=shape[1])
            elif len(shape) == 4:
                a = a.rearrange("p (a b c) -> p a b c", a=shape[1], b=shape[2])
            return a, off + nw

        class Alloc:
            def __init__(self, start, end):
                self.off = start; self.end = end

            def __call__(self, shape, dt):
                a, self.off = view(self.off, shape, dt)
                assert self.off <= self.end, (self.off, self.end)
                return a

        def bank(b, n=512, off=0):
            return psum[:, b * 512 + off:b * 512 + off + n]

        def bank_bf(b):
            return psum[:, b * 512:(b + 1) * 512].bitcast(BF16)

        pg = Prog(nc, reorder=REORDER)

        A0 = Alloc(0, 2048)
        identb = A0([128, 128], BF16); identf = A0([128, 128], F32)
        mask2 = A0([128, 2, 128], BF16); invf = A0([128, 32], F32)
        ones_bf = A0([128, 8], BF16); Esel = A0([128, 4, 128], BF16)
        gcol = A0([128, 40], F32); hb = A0([128, 2], F32)
        posi = A0([128, 32], I32); posf = A0([128, 32], F32)
        ss1 = A0([128, 34], F32); r1 = A0([128, 34], F32)
        ss3 = A0([128, 18], F32); r3 = A0([128, 18], F32)
        ss4 = A0([128, 16], F32); r4 = A0([128, 16], F32)
        ssa = A0([128, 16], F32); na = A0([128, 16], F32)
        ssc = A0([128, 16], F32); ncn = A0([128, 16], F32)
        ssx = A0([128, 16], F32); nxn = A0([128, 16], F32)
        ssy = A0([128, 16], F32); ry = A0([128, 16], F32)
        ssf = A0([128, 16], F32); rf = A0([128, 16], F32)
        rzx = A0([128, 8], F32)
        Zs = A0([128, 2, 8], F32)
        Zh = A0([128, 2, 16], BF16)
        kmT = A0([128, 2, 256], BF16); vm_aug = A0([128, 2, 4, 80], BF16)
        gpre = gcol[:, 0:8]; gmem = gcol[:, 8:16]; gpremlp = gcol[:, 16:24]
        ga = gcol[:, 24:28]; gc = gcol[:, 28:30]; gx = gcol[:, 30:32]; convw = gcol[:, 32:40]
        cosT, o_ = view(2048, [128, 32, 32], F32)
        sinT, o_ = view(o_, [128, 32, 32], F32)
        Gpm, o_ = view(2048, [128, 1024], F32)
        Gpl, o_ = view(o_, [128, 1024], F32)

        A1 = Alloc(4096, ARENA_WORDS)
        qT = A1([128, 4, 2048], BF16); kT = A1([128, 4, 4096], BF16); vT = A1([128, 4, 4096], BF16)
        assert A1.off == 24576
        wqkv = A1([128, 8, 1536], BF16)
        assert A1.off == 30720
        xt = [A1([128, 1024], F32) for _ in range(3)]
        sqs = A1([128, 1024], BF16)
        xn = [A1([128, 1024], BF16) for _ in range(4)]
        hT = [A1([128, 8, 512], BF16) for _ in range(2)]
        qk_sb = [A1([128, 8, 2, 32], F32) for _ in range(4)]
        rtmp = [A1([128, 8, 32], F32) for _ in range(8)]
        qr = [A1([128, 8, 2, 32], BF16) for _ in range(4)]
        wkv = A1([128, 8, 512], BF16)
        memT = A1([128, 8, 256], BF16)

        dcount = [0]

        def load_small(dst, src, eng="sync", stream="ld", key=None):
            key = key or ("c", dcount[0]); dcount[0] += 1
            pg.dma(eng, lambda e: e.dma_start(out=dst, in_=src), writes=[key], stream=stream)
            return key

        def norm_tile(src_ap, slot, ss_col, r_col, skey, xkey, Dn, srckeys):
            pg.op("act", lambda e: e.activation(out=sqs, in_=src_ap, func=AF.Square, accum_out=ss_col),
                  reads=srckeys, writes=["sqs", skey], c=1.0)
            pg.op("act", lambda e: e.activation(out=r_col, in_=ss_col, func=AF.Sqrt, bias=EPS, scale=1.0 / Dn),
                  reads=[skey], writes=[skey + ("r",)], c=0.25)
            pg.op("dve", lambda e: e.reciprocal(out=r_col, in_=r_col), reads=[skey + ("r",)], writes=[skey + ("r",)], c=0.15)
            pg.op("act", lambda e: e.activation(out=xn[slot], in_=src_ap, func=AF.Copy, scale=r_col),
                  reads=srckeys + [skey + ("r",)], writes=[("xn", slot)], c=1.0)

        def transpose_to(dstT, col0, slot, gcols_ap, tbank, dkey):
            pb = bank_bf(tbank)
            for kc in range(8):
                pg.op("pe", lambda e, kc=kc: e.transpose(out=pb[:, kc * 128:(kc + 1) * 128],
                                                        in_=xn[slot][:, kc * 128:(kc + 1) * 128], identity=identb),
                      reads=[("xn", slot), "identb"], writes=[("ps", tbank)], c=0.09)
            pg.op("dve", lambda e: e.tensor_tensor(out=dstT[:, :, col0:col0 + 128],
                                                   in0=pb.rearrange("p (a b) -> p a b", a=8),
                                                   in1=gcols_ap.unsqueeze(2).broadcast_to([128, 8, 128]), op=ALU.mult),
                  reads=[("ps", tbank), "gcol"], writes=[dkey], c=1.2)

        def dump(name, ap, shape, dt):
            d = nc.dram_tensor("dbg_" + name, list(shape), dt, kind="ExternalOutput").ap()
            dbg_out[name] = d
            pg.barrier()
            return pg.dma("sync", lambda e: e.dma_start(out=d, in_=ap), stream="st")
        finals = []
        pg.dma("pool", lambda e: e.dma_start(out=identb, in_=identd[:, :]), writes=["identb"], stream="ldw")
        pg.dma("pool", lambda e: e.dma_start(out=mask2.rearrange("p a b -> p (a b)"), in_=maskd[:, :]), writes=["mask2"], stream="ldw")
        pg.dma("sync", lambda e: e.dma_start(out=identf, in_=identd[:, :]), writes=["identf"], stream="ld")
        pg.dma("sync", lambda e: e.dma_start(out=invf, in_=invfd[:, :]), writes=["invf"], stream="ld")
        pg.dma("sync", lambda e: e.dma_start(out=gcol, in_=gcold[:, :]), writes=["gcol"], stream="ld")
        pg.dma("sync", lambda e: e.dma_start(out=hb[:, 0:1], in_=hbd[:, :]), writes=["hb"], stream="ld")
        pg.dma("sync", lambda e: e.dma_start(out=posi, in_=posT[:, :]), writes=["posi"], stream="ld")
        pg.dma("pool", lambda e: e.dma_start(out=Esel[0:8].rearrange("p a b -> p (a b)"), in_=Ed[:, :]), writes=["Esel"], stream="ldw")
        pg.op("dve", lambda e: e.memset(ones_bf, 1.0), writes=["ones"])
        pg.op("dve", lambda e: e.memset(vm_aug.rearrange("p a b c -> p (a b c)"), 1.0), writes=["vm_aug"])
        for kc in range(8):
            pg.dma("pool", lambda e, kc=kc: e.dma_start(out=wkv[:, kc, :], in_=w_kv[kc * 128:(kc + 1) * 128, :]),
                   writes=[("wkv", kc)], stream="ldw")
        for kc in range(8):
            pg.dma("pool", lambda e, kc=kc: e.dma_start(out=wqkv[:, kc, :], in_=w_in[kc * 128:(kc + 1) * 128, 0:1536]),
                   writes=[("wqkv", kc)], stream="ldw")
        WQKV = [("wqkv", kc) for kc in range(8)]
        WKV = [("wkv", kc) for kc in range(8)]

        t_off = 30720 + 3072 + 512 + 2048
        angv, t_o = view(t_off, [128, 32, 32], F32)
        kfv, t_o = view(t_o, [128, 32, 32], F32)
        kiv, t_o = view(t_o, [128, 32, 32], I32)
        mmv, t_o = view(t_o, [128, 32, 32], F32)
        TK = [("hT", 0), ("hT", 1)]
        pg.op("dve", lambda e: e.tensor_copy(out=posf, in_=posi), reads=["posi"], writes=["posf"])
        pg.op("dve", lambda e: e.tensor_tensor(out=angv, in0=posf.unsqueeze(2).broadcast_to([128, 32, 32]),
                                               in1=invf.unsqueeze(1).broadcast_to([128, 32, 32]), op=ALU.mult),
              reads=["posf", "invf"], writes=TK)

        def range_reduce_sin(dst, shift):
            pg.op("dve", lambda e: e.tensor_scalar(out=mmv, in0=angv, scalar1=shift, scalar2=None, op0=ALU.add), reads=TK, writes=TK)
            pg.op("dve", lambda e: e.tensor_scalar(out=kfv, in0=mmv, scalar1=1.0 / TWO_PI, scalar2=None, op0=ALU.mult), reads=TK, writes=TK)
            pg.op("dve", lambda e: e.tensor_copy(out=kiv, in_=kfv), reads=TK, writes=TK)
            pg.op("dve", lambda e: e.tensor_copy(out=kfv, in_=kiv), reads=TK, writes=TK)
            pg.op("dve", lambda e: e.scalar_tensor_tensor(out=mmv, in0=kfv, scalar=-TWO_PI, in1=mmv, op0=ALU.mult, op1=ALU.add), reads=TK, writes=TK)
            pg.op("dve", lambda e: e.tensor_single_scalar(out=kfv, in_=mmv, scalar=PI, op=ALU.is_gt), reads=TK, writes=TK)
            pg.op("dve", lambda e: e.scalar_tensor_tensor(out=mmv, in0=kfv, scalar=-TWO_PI, in1=mmv, op0=ALU.mult, op1=ALU.add), reads=TK, writes=TK)
            pg.op("dve", lambda e: e.tensor_single_scalar(out=kfv, in_=mmv, scalar=-PI, op=ALU.is_lt), reads=TK, writes=TK)
            pg.op("dve", lambda e: e.scalar_tensor_tensor(out=mmv, in0=kfv, scalar=TWO_PI, in1=mmv, op0=ALU.mult, op1=ALU.add), reads=TK, writes=TK)
            pg.op("act", lambda e: e.activation(out=dst, in_=mmv, func=AF.Sin), reads=TK, writes=TK + ["tables"])

        range_reduce_sin(sinT, 0.0)
        range_reduce_sin(cosT, PI / 2)

        xcount = [0]

        def load_x(src_ap):
            slot = xcount[0] % len(xt); xcount[0] += 1
            pg.dma("sync", lambda e: e.dma_start(out=xt[slot], in_=src_ap), writes=[("xt", slot)], stream="ldx")
            return slot

        for m in range(2):
            s = load_x(memd[m * 128:(m + 1) * 128, :])
            norm_tile(xt[s], m, ss1[:, 32 + m:33 + m], r1[:, 32 + m:33 + m], ("s1", 32 + m), None, D, [("xt", s)])
            transpose_to(memT, m * 128, m, gmem, 0, ("memT", m))
        MEMT = [("memT", 0), ("memT", 1)]
        for c in range(2):
            for kc in range(8):
                pg.op("pe", lambda e, c=c, kc=kc: e.matmul(bank(1, 256), lhsT=wkv[:, kc, c * 128:(c + 1) * 128], rhs=memT[:, kc, :],
                                                           start=(kc == 0), stop=(kc == 7)),
                      reads=WKV + MEMT, writes=[("ps", 1)])
            pg.op("act", lambda e, c=c: e.copy(out=kmT[:, c, :], in_=bank(1, 256)), reads=[("ps", 1)], writes=["kmT"])
        for m in range(2):
            for kc in range(8):
                pg.op("pe", lambda e, m=m, kc=kc: e.matmul(bank(2, 256), lhsT=memT[:, kc, m * 128:(m + 1) * 128], rhs=wkv[:, kc, 256:512],
                                                           start=(kc == 0), stop=(kc == 7)),
                      reads=WKV + MEMT, writes=[("ps", 2)])
            pg.op("act", lambda e, m=m: e.copy(out=vm_aug[:, m, :, 0:64], in_=bank(2, 256).rearrange("p (a b) -> p a b", a=4)),
                  reads=[("ps", 2)], writes=["vm_aug"])

        if upto == "p0":
            f1 = dump("kmT", kmT, [128, 2, 256], BF16); f2 = dump("vm", vm_aug.rearrange("p a b c -> p (a b c)"), [128, 640], BF16)
            f3 = dump("cos", cosT, [128, 32, 32], F32); f4 = dump("sin", sinT, [128, 32, 32], F32)
            pg.emit(final_waits=[f1, f2, f3, f4])
            return nc, dbg_out
        mmb = [0]
        nrot = [3]

        def next_bank():
            b = 1 + (mmb[0] % nrot[0]); mmb[0] += 1
            return b
        trb = [0]

        def rope_and_store(pbank, gt, dstT, col0, slotk):
            sl = slotk % 4
            rs = (slotk % 2) * 4
            pg.op("act", lambda e: e.copy(out=qk_sb[sl].rearrange("p a b c -> p (a b c)"), in_=bank(pbank)),
                  reads=[("ps", pbank)], writes=[("qk_sb", sl)])
            cs = cosT[:, gt, :].unsqueeze(1).broadcast_to([128, 8, 32])
            sn = sinT[:, gt, :].unsqueeze(1).broadcast_to([128, 8, 32])
            t1 = qk_sb[sl][:, :, 0, :]; t2 = qk_sb[sl][:, :, 1, :]
            pg.op("dve", lambda e: e.tensor_tensor(out=rtmp[rs + 0], in0=t1, in1=cs, op=ALU.mult), reads=[("qk_sb", sl), "tables"], writes=[("rt", rs + 0)], c=0.4)
            pg.op("dve", lambda e: e.tensor_tensor(out=rtmp[rs + 1], in0=t2, in1=sn, op=ALU.mult), reads=[("qk_sb", sl), "tables"], writes=[("rt", rs + 1)], c=0.4)
            pg.op("dve", lambda e: e.tensor_tensor(out=qr[sl][:, :, 0, :], in0=rtmp[rs + 0], in1=rtmp[rs + 1], op=ALU.subtract),
                  reads=[("rt", rs + 0), ("rt", rs + 1)], writes=[("qr", sl, 0)], c=0.4)
            pg.op("pool", lambda e: e.tensor_tensor(out=rtmp[rs + 2], in0=t1, in1=sn, op=ALU.mult), reads=[("qk_sb", sl), "tables"], writes=[("rt", rs + 2)])
            pg.op("pool", lambda e: e.tensor_tensor(out=rtmp[rs + 3], in0=t2, in1=cs, op=ALU.mult), reads=[("qk_sb", sl), "tables"], writes=[("rt", rs + 3)])
            pg.op("pool", lambda e: e.tensor_tensor(out=qr[sl][:, :, 1, :], in0=rtmp[rs + 2], in1=rtmp[rs + 3], op=ALU.add),
                  reads=[("rt", rs + 2), ("rt", rs + 3)], writes=[("qr", sl, 1)])
            tb = 4 + (trb[0] % 2); trb[0] += 1
            pb = bank_bf(tb)
            qflat = qr[sl].rearrange("p a b c -> p (a b c)")
            for c in range(4):
                pg.op("pe", lambda e, c=c: e.transpose(out=pb[:, c * 128:(c + 1) * 128], in_=qflat[:, c * 128:(c + 1) * 128], identity=identb),
                      reads=[("qr", sl, 0), ("qr", sl, 1), "identb"], writes=[("ps", tb)], c=0.09)
            pg.op("act", lambda e: e.copy(out=dstT[:, :, col0:col0 + 128], in_=pb[:, 0:512].rearrange("p (a b) -> p a b", a=4)),
                  reads=[("ps", tb)], writes=[("qkT", gt, id(dstT))])

        ropec = [0]
        def front1(nb):
            own = nb >= 4
            hs = nb % 2
            for ti in range(4):
                gt = nb * 4 + ti
                src = xo[(gt - 16) * 128:(gt - 15) * 128, :] if own else xh[gt * 128:(gt + 1) * 128, :]
                s = load_x(src)
                norm_tile(xt[s], gt % 4, ss1[:, gt:gt + 1], r1[:, gt:gt + 1], ("s1", gt), None, D, [("xt", s)])
                transpose_to(hT[hs], ti * 128, gt % 4, gpre, 0, ("hT", hs))

        front1(0)
        for nb in range(8):
            own = nb >= 4
            hs = nb % 2
            if nb + 1 < 8:
                front1(nb + 1)
            HK = [("hT", hs)]
            for ti in range(4):
                gt = nb * 4 + ti
                for which in ([1, 0] if own else [1]):
                    pb = next_bank()
                    for kc in range(8):
                        pg.op("pe", lambda e, kc=kc, pb=pb, which=which, ti=ti: e.matmul(
                            bank(pb), lhsT=hT[hs][:, kc, ti * 128:(ti + 1) * 128], rhs=wqkv[:, kc, which * 512:(which + 1) * 512],
                            start=(kc == 0), stop=(kc == 7)), reads=HK + WQKV, writes=[("ps", pb)])
                    if which == 1:
                        rope_and_store(pb, gt, kT, gt * 128, ropec[0])
                    else:
                        rope_and_store(pb, gt, qT, (gt - 16) * 128, ropec[0])
                    ropec[0] += 1
            for j in range(4):
                pb = next_bank()
                for kc in range(8):
                    pg.op("pe", lambda e, kc=kc, pb=pb, j=j: e.matmul(
                        bank(pb), lhsT=wqkv[:, kc, 1024 + j * 128:1024 + (j + 1) * 128], rhs=hT[hs][:, kc, :],
                        start=(kc == 0), stop=(kc == 7)), reads=HK + WQKV, writes=[("ps", pb)])
                pg.op("act" if j % 2 == 0 else "dve",
                      (lambda e, pb=pb, j=j: e.copy(out=vT[:, j, nb * 512:(nb + 1) * 512], in_=bank(pb))) if j % 2 == 0 else
                      (lambda e, pb=pb, j=j: e.tensor_copy(out=vT[:, j, nb * 512:(nb + 1) * 512], in_=bank(pb))),
                      reads=[("ps", pb)], writes=[("vT", nb, j)])

        if "qkv" in debug:
            finals.append(dump("qT", qT, [128, 4, 2048], BF16))
            finals.append(dump("kT", kT, [128, 4, 4096], BF16))
            finals.append(dump("vT", vT, [128, 4, 4096], BF16))

        if upto == "p1":
            pg.emit(final_waits=finals)
            return nc, dbg_out
        pg.barrier()
        A2 = Alloc(24576, ARENA_WORDS)
        ya_acc = A2([128, 4, 2048], F32)
        ZT_acc = A2([128, 2048], F32)
        PT = [A2([128, 4, 128], BF16) for _ in range(8)]
        Vaug = [A2([128, 8, 80], BF16) for _ in range(3)]
        Obf = [A2([128, 8, 64], BF16) for _ in range(2)]
        AT = Alloc(2048, 4096)
        rZ = AT([128, 512], F32)
        sqA = [AT([128, 512], BF16) for _ in range(4)]
        rZh = AT([128, 512], BF16); rZl = AT([128, 512], BF16)
        assert A2.off <= 38400, A2.off
        OB = {0: 7 * 512, 1: 0}
        SSA0 = 300
        A2.off = 38400
        ya_T = A2([128, 4, 2048], BF16)
        wcx = A2([128, 8, 1024], BF16)
        wout = A2([128, 8, 1024], BF16)
        assert A2.off == 50688
        for kc in range(8):
            pg.dma("pool", lambda e, kc=kc: e.dma_start(out=wcx[:, kc, :], in_=w_in[kc * 128:(kc + 1) * 128, 1536:2560]),
                   writes=[("wcx", kc)], stream="ldw")
        for kc in range(8):
            pg.dma("pool", lambda e, kc=kc: e.dma_start(out=wout[:, kc, :], in_=w_out[kc * 128:(kc + 1) * 128, :]),
                   writes=[("wout", kc)], stream="ldw")
        WCX = [("wcx", kc) for kc in range(8)]
        WOUT = [("wout", kc) for kc in range(8)]
        for i in range(3):
            pg.op("pool", lambda e, i=i: e.memset(Vaug[i].rearrange("p a b -> p (a b)"), 1.0), writes=[("Vaug", i)])

        vcount = [0]

        def build_v(kbase, d):
            i = vcount[0] % 3; vcount[0] += 1
            tb = 6
            pb = bank_bf(tb)
            for j in range(4):
                pg.op("pe", lambda e, j=j: e.transpose(out=pb[:, j * 128:(j + 1) * 128],
                                                      in_=vT[:, j, kbase:kbase + 127 * d + 1:d], identity=identb),
                      reads=["identb"], writes=[("ps", tb)], c=0.09)
            pg.op("dve", lambda e: e.tensor_copy(out=Vaug[i][:, :, 0:64], in_=pb[:, 0:512].rearrange("p (a b) -> p a b", a=8)),
                  reads=[("ps", tb)], writes=[("Vaug", i)])
            return i

        ptc = [0]
        grp = [0]
        patterns = [(1, [(128 * t, t) for t in range(16)]),
                    (4, [(512 * u + c, u) for c in range(4) for u in range(4)]),
                    (16, [(r, 0) for r in range(16)])]
        for (d, groups) in patterns:
            prev_v = None
            for gi, (q0, chain) in enumerate(groups):
                kdiag = 2048 + q0
                kprev = kdiag - 128 * d
                if d == 16 or chain == 0 or prev_v is None:
                    vprev = build_v(kprev, d)
                else:
                    vprev = prev_v
                vdiag = build_v(kdiag, d)
                prev_v = vdiag
                halo_prev = kprev < 2048
                if upto == "s1":
                    pg.barrier()
                    pg.emit(final_waits=[pg.dma("sync", lambda e: e.dma_start(out=outd[0:128, 0:8], in_=hb[:, 0:2].bitcast(F32)[:, 0:2].broadcast_to([128, 2]) if False else ss1[:, 0:8]), stream="st")])
                    return nc, dbg_out
                pts = {}
                for kt, kb in ((0, kprev), (1, kdiag)):
                    sbs = [next_bank(), next_bank()]
                    for par in range(2):
                        sb = sbs[par]
                        po = par * 64
                        for hh in range(4):
                            h = hh * 2 + par
                            pg.op("pe", lambda e, sb=sb, hh=hh, h=h, po=po, kb=kb: e.matmul(
                                bank(sb, 128, hh * 128), lhsT=kT[po:po + 64, h // 2, kb:kb + 127 * d + 1:d],
                                rhs=qT[po:po + 64, h // 2, q0:q0 + 127 * d + 1:d], start=True, stop=True),
                                reads=[], writes=[("ps", sb)], c=0.085)
                    for hg in range(2):
                        sb = sbs[hg]
                        pi = ptc[0] % 8; ptc[0] += 1
                        pts[(kt, hg)] = pi
                        ptf = PT[pi].rearrange("p a b -> p (a b)")
                        if kt == 0 and halo_prev:
                            pg.op("act", lambda e, sb=sb, ptf=ptf: e.activation(out=ptf, in_=bank(sb), func=AF.Exp, bias=hb[:, 0:1], scale=0.125),
                                  reads=[("ps", sb), "hb"], writes=[("PT", pi)])
                        else:
                            pg.op("act", lambda e, sb=sb, ptf=ptf: e.activation(out=ptf, in_=bank(sb), func=AF.Exp, scale=0.125),
                                  reads=[("ps", sb)], writes=[("PT", pi)])
                        meng = "dve" if (kt + hg) % 2 == 0 else "pool"
                        pg.op(meng, lambda e, pi=pi, kt=kt: e.tensor_tensor(out=PT[pi], in0=PT[pi],
                                                                          in1=mask2[:, kt, :].unsqueeze(1).broadcast_to([128, 4, 128]), op=ALU.mult),
                              reads=[("PT", pi), "mask2"], writes=[("PT", pi)])
                if upto == "s2":
                    pg.barrier()
                    pg.emit(final_waits=[pg.dma("sync", lambda e: e.dma_start(out=outd[0:128, 0:8], in_=hb[:, 0:2].bitcast(F32)[:, 0:2].broadcast_to([128, 2]) if False else ss1[:, 0:8]), stream="st")])
                    return nc, dbg_out
                for hg in range(2):
                    for hh in range(4):
                        h = hg * 4 + hh
                        for kt, vi in ((0, vprev), (1, vdiag)):
                            pi = pts[(kt, h % 2)]
                            pg.op("pe", lambda e, hg=hg, hh=hh, h=h, pi=pi, vi=vi, kt=kt: e.matmul(
                                psum[:, OB[hg] + hh * 65:OB[hg] + hh * 65 + 65],
                                lhsT=PT[pi][:, h // 2, :], rhs=Vaug[vi][:, h, 0:65], start=(kt == 0), stop=(kt == 1)),
                                reads=[("PT", pi), ("Vaug", vi)], writes=[("ps", 7 if hg == 0 else 0)], c=0.06)
                if upto == "s3":
                    pg.barrier()
                    pg.emit(final_waits=[pg.dma("sync", lambda e: e.dma_start(out=outd[0:128, 0:8], in_=hb[:, 0:2].bitcast(F32)[:, 0:2].broadcast_to([128, 2]) if False else ss1[:, 0:8]), stream="st")])
                    return nc, dbg_out
                osl = grp[0] % 2
                for hg in range(2):
                    ov = psum[:, OB[hg]:OB[hg] + 260].rearrange("p (a b) -> p a b", a=4)
                    okey = ("ps", 7 if hg == 0 else 0)
                    pg.op("act", lambda e, hg=hg, ov=ov: e.copy(out=Obf[osl][:, hg * 4:(hg + 1) * 4, :], in_=ov[:, :, 0:64]),
                          reads=[okey], writes=[("Obf", osl), okey])
                    pg.op("dve", lambda e, hg=hg, ov=ov: e.tensor_copy(out=Zs[:, osl, hg * 4:(hg + 1) * 4], in_=ov[:, :, 64]),
                          reads=[okey], writes=[("Zs", osl), okey])
                if upto == "s4":
                    pg.barrier()
                    pg.emit(final_waits=[pg.dma("sync", lambda e: e.dma_start(out=outd[0:128, 0:8], in_=hb[:, 0:2].bitcast(F32)[:, 0:2].broadcast_to([128, 2]) if False else ss1[:, 0:8]), stream="st")])
                    return nc, dbg_out
                tb = 4 + (trb[0] % 2); trb[0] += 1
                pb = bank_bf(tb)
                of = Obf[osl].rearrange("p a b -> p (a b)")
                for c in range(4):
                    pg.op("pe", lambda e, c=c: e.transpose(out=pb[:, c * 128:(c + 1) * 128], in_=of[:, c * 128:(c + 1) * 128], identity=identb),
                          reads=[("Obf", osl), "identb"], writes=[("ps", tb)], c=0.09)
                pg.op("dve", lambda e: e.tensor_copy(out=Zh[:, osl, 0:8], in_=Zs[:, osl, :]), reads=[("Zs", osl)], writes=[("Zh", osl, 0)], c=0.15)
                pg.op("dve", lambda e: e.tensor_tensor(out=Zh[:, osl, 8:16], in0=Zs[:, osl, :], in1=Zh[:, osl, 0:8], op=ALU.subtract),
                      reads=[("Zs", osl), ("Zh", osl, 0)], writes=[("Zh", osl, 1)], c=0.15)
                for part in range(2):
                    pg.op("pe", lambda e, part=part: e.matmul(psum[0:8, tb * 512 + 256:tb * 512 + 384], lhsT=Zh[:, osl, part * 8:(part + 1) * 8], rhs=identb,
                                                             start=(part == 0), stop=(part == 1)),
                          reads=[("Zh", osl, 0), ("Zh", osl, 1), "identb"], writes=[("ps", tb)], c=0.09)
                accv = ya_acc[:, :, q0:q0 + 127 * d + 1:d]
                zv = ZT_acc[0:8, q0:q0 + 127 * d + 1:d]
                if d == 1:
                    akeys = [("acc", q0 // 512)]
                elif d == 4:
                    akeys = [("acc", q0 // 512)]
                else:
                    akeys = [("acc", i) for i in range(4)]
                psv = pb[:, 0:512].rearrange("p (a b) -> p a b", a=4)
                pz = psum[0:8, tb * 512 + 256:tb * 512 + 384]
                if d == 1:
                    pg.op("dve", lambda e: e.tensor_copy(out=accv, in_=psv), reads=[("ps", tb)], writes=akeys)
                    pg.op("dve", lambda e: e.tensor_copy(out=zv, in_=pz), reads=[("ps", tb)], writes=[k + ("z",) for k in akeys])
                else:
                    pg.op("dve", lambda e: e.tensor_tensor(out=accv, in0=accv, in1=psv, op=ALU.add), reads=[("ps", tb)] + akeys, writes=akeys)
                    pg.op("dve", lambda e: e.tensor_tensor(out=zv, in0=zv, in1=pz, op=ALU.add),
                          reads=[("ps", tb)] + [k + ("z",) for k in akeys], writes=[k + ("z",) for k in akeys])
                grp[0] += 1
                if upto is not None and upto.startswith("g") and grp[0] == int(upto[1:]):
                    finals.append(dump("acc", ya_acc, [128, 4, 2048], F32))
                    finals.append(dump("ZT", ZT_acc[0:8, :], [8, 2048], F32))
                    pg.emit(final_waits=finals)
                    return nc, dbg_out

        for bl in range(4):
            cs_ = slice(bl * 512, (bl + 1) * 512)
            pg.op("dve", lambda e, cs_=cs_: e.reciprocal(out=rZ[0:8, :], in_=ZT_acc[0:8, cs_]),
                  reads=[("acc", bl, "z")], writes=["rZ"])
            pg.op("dve", lambda e: e.tensor_copy(out=rZh[0:8, :], in_=rZ[0:8, :]), reads=["rZ"], writes=["rZhl"])
            pg.op("dve", lambda e: e.tensor_tensor(out=rZl[0:8, :], in0=rZ[0:8, :], in1=rZh[0:8, :], op=ALU.subtract), reads=["rZ", "rZhl"], writes=["rZhl"])
            for c in range(4):
                sb = next_bank()
                pg.op("pe", lambda e, c=c, sb=sb: e.matmul(bank(sb), lhsT=Esel[0:8, c, :], rhs=rZh[0:8, :], start=True, stop=False),
                      reads=["rZhl", "Esel"], writes=[("ps", sb)], c=0.25)
                pg.op("pe", lambda e, c=c, sb=sb: e.matmul(bank(sb), lhsT=Esel[0:8, c, :], rhs=rZl[0:8, :], start=False, stop=True),
                      reads=["rZhl", "Esel"], writes=[("ps", sb)], c=0.25)
                pg.op("dve", lambda e, c=c, sb=sb, cs_=cs_: e.tensor_tensor(out=ya_acc[:, c, cs_], in0=ya_acc[:, c, cs_], in1=bank(sb), op=ALU.mult),
                      reads=[("ps", sb), ("acc", bl)], writes=[("accn", bl, c)])
                sl = c
                pg.op("act", lambda e, c=c, sl=sl, cs_=cs_: e.activation(out=sqA[sl], in_=ya_acc[:, c, cs_], func=AF.Square),
                      reads=[("accn", bl, c)], writes=[("sqA", sl)])
                pg.op("act", lambda e, c=c, cs_=cs_: e.activation(out=ya_T[:, c, cs_], in_=ya_acc[:, c, cs_], func=AF.Copy, scale=ga[:, c:c + 1]),
                      reads=[("accn", bl, c), "gcol"], writes=[("ya_T", bl, c)])
            for ti in range(4):
                for c in range(4):
                    pg.op("pe", lambda e, c=c, ti=ti: e.matmul(psum[:, SSA0 + ti:SSA0 + ti + 1], lhsT=sqA[c][:, ti * 128:(ti + 1) * 128],
                                                              rhs=ones_bf[:, 0:1], start=(c == 0), stop=(c == 3)),
                          reads=[("sqA", c), "ones"], writes=[("ps", 0)], c=0.06)
            pg.op("dve", lambda e, bl=bl: e.tensor_copy(out=ssa[:, bl * 4:(bl + 1) * 4], in_=psum[:, SSA0:SSA0 + 4]),
                  reads=[("ps", 0)], writes=[("ssa", bl)])
            pg.op("act", lambda e, bl=bl: e.activation(out=na[:, bl * 4:(bl + 1) * 4], in_=ssa[:, bl * 4:(bl + 1) * 4], func=AF.Sqrt, bias=EPS, scale=1.0 / 512),
                  reads=[("ssa", bl)], writes=[("na", bl)])
            pg.op("dve", lambda e, bl=bl: e.reciprocal(out=na[:, bl * 4:(bl + 1) * 4], in_=na[:, bl * 4:(bl + 1) * 4]),
                  reads=[("na", bl)], writes=[("na", bl)])

        if "attn" in debug:
            finals.append(dump("yaT", ya_T, [128, 4, 2048], BF16))
            finals.append(dump("na", na, [128, 16], F32))

        if upto == "p2":
            pg.emit(final_waits=finals)
            return nc, dbg_out
        pg.barrier()
        pg.dma("sync", lambda e: e.dma_start(out=Gpm, in_=gpmd.partition_broadcast(128)), writes=["Gpm"], stream="ld")
        pg.dma("sync", lambda e: e.dma_start(out=Gpl, in_=gpld.partition_broadcast(128)), writes=["Gpl"], stream="ld")
        x1, o_ = view(4096, [128, 16, 1024], F32)
        nrot[0] = 2
        OX0 = 3 * 512
        SSC0 = 3 * 512 + 480
        OP0 = 4 * 512
        yxps = bank_bf(3)[:, 640:896]
        A3 = Alloc(20480, 38400)
        hT3_ = [A3([128, 8, 512], BF16) for _ in range(2)]
        sqs3 = A3([128, 1024], BF16)
        xn3 = [A3([128, 1024], BF16) for _ in range(2)]
        c_sb = [A3([128, 512], F32) for _ in range(2)]
        zb = A3([128, 2, 516], F32)
        ytap = [A3([128, 512], F32) for _ in range(2)]
        sqc = [A3([128, 512], BF16) for _ in range(2)]
        ycT_ = [A3([128, 2, 512], BF16) for _ in range(2)]
        xqT = A3([128, 2, 512], BF16)
        PxT = [A3([128, 512], BF16) for _ in range(8)]
        yxf = [A3([128, 4, 64], F32) for _ in range(2)]
        yxb = [A3([128, 256], BF16) for _ in range(2)]
        yxT_ = [A3([128, 2, 512], BF16) for _ in range(2)]
        yv = [A3([128, 1024], F32) for _ in range(2)]
        sqs_, xn_ = sqs3, xn3

        def norm_tile3(src_ap, slot, ss_col, r_col, skey, srckeys):
            pg.op("act", lambda e: e.activation(out=sqs_, in_=src_ap, func=AF.Square, accum_out=ss_col),
                  reads=srckeys, writes=["sqs3", skey], c=1.0)
            pg.op("act", lambda e: e.activation(out=r_col, in_=ss_col, func=AF.Sqrt, bias=EPS, scale=1.0 / D),
                  reads=[skey], writes=[skey + ("r",)], c=0.25)
            pg.op("dve", lambda e: e.reciprocal(out=r_col, in_=r_col), reads=[skey + ("r",)], writes=[skey + ("r",)], c=0.15)
            pg.op("act", lambda e: e.activation(out=xn_[slot], in_=src_ap, func=AF.Copy, scale=r_col),
                  reads=srckeys + [skey + ("r",)], writes=[("xn3", slot)], c=1.0)

        def transpose_to3(dstT, col0, slot, gcols_ap, tbank, dkey):
            pb = bank_bf(tbank)
            for kc in range(8):
                pg.op("pe", lambda e, kc=kc: e.transpose(out=pb[:, kc * 128:(kc + 1) * 128],
                                                        in_=xn_[slot][:, kc * 128:(kc + 1) * 128], identity=identb),
                      reads=[("xn3", slot), "identb"], writes=[("ps", tbank)], c=0.09)
            pg.op("dve", lambda e: e.tensor_tensor(out=dstT[:, :, col0:col0 + 128],
                                                   in0=pb.rearrange("p (a b) -> p a b", a=8),
                                                   in1=gcols_ap.unsqueeze(2).broadcast_to([128, 8, 128]), op=ALU.mult),
                  reads=[("ps", tbank), "gcol"], writes=[dkey], c=1.2)

        def conv_cu(ncols, zoff, hT3, hkey):
            for j in range(2):
                pb = next_bank()
                for kc in range(8):
                    pg.op("pe", lambda e, kc=kc, pb=pb, j=j: e.matmul(bank(pb, ncols), lhsT=wcx[:, kc, 256 + j * 128:256 + (j + 1) * 128],
                                                                    rhs=hT3[:, kc, 0:ncols], start=(kc == 0), stop=(kc == 7)),
                          reads=[hkey] + WCX, writes=[("ps", pb)])
                pg.op("act", lambda e, pb=pb, j=j: e.copy(out=c_sb[j][:, 0:ncols], in_=bank(pb, ncols)), reads=[("ps", pb)], writes=[("c_sb", j)])
            for j in range(2):
                pb = next_bank()
                for kc in range(8):
                    pg.op("pe", lambda e, kc=kc, pb=pb, j=j: e.matmul(bank(pb, ncols), lhsT=wcx[:, kc, 512 + j * 128:512 + (j + 1) * 128],
                                                                    rhs=hT3[:, kc, 0:ncols], start=(kc == 0), stop=(kc == 7)),
                          reads=[hkey] + WCX, writes=[("ps", pb)])
                pg.op("dve", lambda e, pb=pb, j=j: e.tensor_tensor(out=zb[:, j, zoff:zoff + ncols], in0=c_sb[j][:, 0:ncols], in1=bank(pb, ncols), op=ALU.mult),
                      reads=[("ps", pb), ("c_sb", j)], writes=[("z", j)])

        opair = [0]
        def outproj(bl, tiles=(0, 1, 2, 3)):
                if bl < 0:
                    return
                ycT = ycT_[bl % 2]; yxT = yxT_[bl % 2]
                for ti in tiles:
                    t = bl * 4 + ti
                    ysl = t % 2
                    groups3 = [([(ya_T[:, c, t * 128:(t + 1) * 128], c) for c in range(4)], na[:, t:t + 1], [("ya_T", bl, c) for c in range(4)] + [("na", bl)]),
                               ([(ycT[:, j, ti * 128:(ti + 1) * 128], 4 + j) for j in range(2)], ncn[:, t:t + 1], [("ycT", bl % 2, 0), ("ycT", bl % 2, 1), ("ncn", bl)]),
                               ([(yxT[:, j, ti * 128:(ti + 1) * 128], 6 + j) for j in range(2)], nxn[:, t:t + 1], [("yxT", bl % 2, ti), ("nxn", t)])]
                    for gi3, (lhs_list, nscale, rkeys) in enumerate(groups3):
                        pr = opair[0] % 2; opair[0] += 1
                        pbase = OP0 + pr * 1024
                        for nh in range(2):
                            for ii, (lh, wc) in enumerate(lhs_list):
                                pg.op("pe", lambda e, lh=lh, wc=wc, nh=nh, ii=ii, pbase=pbase, n=len(lhs_list): e.matmul(
                                    psum[:, pbase + nh * 512:pbase + (nh + 1) * 512], lhsT=lh, rhs=wout[:, wc, nh * 512:(nh + 1) * 512],
                                    start=(ii == 0), stop=(ii == n - 1)), reads=rkeys + WOUT, writes=[("psP", pr)])
                        pv = psum[:, pbase:pbase + 1024]
                        if gi3 == 0:
                            pg.op("act", lambda e, pv=pv, nscale=nscale, ysl=ysl: e.activation(out=yv[ysl], in_=pv, func=AF.Copy, scale=nscale),
                                  reads=[("psP", pr)] + rkeys, writes=[("yv", ysl)], c=1.0)
                        else:
                            pg.op("dve", lambda e, pv=pv, nscale=nscale, ysl=ysl: e.scalar_tensor_tensor(out=yv[ysl], in0=pv, scalar=nscale, in1=yv[ysl], op0=ALU.mult, op1=ALU.add),
                                  reads=[("psP", pr), ("yv", ysl)] + rkeys, writes=[("yv", ysl)], c=1.2)
                    pg.op("act", lambda e, ysl=ysl, t=t: e.activation(out=sqs_, in_=yv[ysl], func=AF.Square, accum_out=ssy[:, t:t + 1]),
                          reads=[("yv", ysl)], writes=["sqs3", ("ssy", t)], c=1.0)
                    pg.op("act", lambda e, t=t: e.activation(out=ry[:, t:t + 1], in_=ssy[:, t:t + 1], func=AF.Sqrt, bias=EPS, scale=1.0 / D),
                          reads=[("ssy", t)], writes=[("ry", t)])
                    pg.op("dve", lambda e, t=t: e.reciprocal(out=ry[:, t:t + 1], in_=ry[:, t:t + 1]), reads=[("ry", t)], writes=[("ry", t)])
                    pg.op("dve", lambda e, ysl=ysl, t=t: e.scalar_tensor_tensor(out=yv[ysl], in0=yv[ysl], scalar=ry[:, t:t + 1], in1=Gpm, op0=ALU.mult, op1=ALU.mult),
                          reads=[("yv", ysl), ("ry", t), "Gpm"], writes=[("yv", ysl)], c=1.2)
                    pg.op("pool", lambda e, ysl=ysl, t=t: e.tensor_tensor(out=x1[:, t, :], in0=x1[:, t, :], in1=yv[ysl], op=ALU.add),
                          reads=[("yv", ysl), ("x1", t)], writes=[("x1", t)], c=2.4)


        def front3(bl):
            for ti in range(4):
                t = bl * 4 + ti
                pg.dma("sync", lambda e, t=t: e.dma_start(out=x1[:, t, :], in_=xo[t * 128:(t + 1) * 128, :]), writes=[("x1", t)], stream="ldx")
                sl = t % 2
                norm_tile3(x1[:, t, :], sl, ss3[:, t:t + 1], r3[:, t:t + 1], ("s3", t), [("x1", t)])
                transpose_to3(hT3_[bl % 2], ti * 128, sl, gpre, 0, ("hT3", bl % 2))

        def prologue3():
            pg.dma("sync", lambda e: e.dma_start(out=yv[0], in_=xh[15 * 128:16 * 128, :]), writes=[("yv", 0)], stream="ldx")
            norm_tile3(yv[0], 0, ss3[:, 16:17], r3[:, 16:17], ("s3", 16), [("yv", 0)])
            transpose_to3(hT3_[1], 0, 0, gpre, 0, ("hT3", 1))
            conv_cu(128, 4, hT3_[1], ("hT3", 1))
            for j in range(2):
                pg.op("pool", lambda e, j=j: e.tensor_copy(out=zb[:, j, 0:2], in_=zb[:, j, 130:132]), reads=[("z", j)], writes=[("z", j)])


        front3(0)
        prologue3()
        for bl in range(4):
            ycT = ycT_[bl % 2]; yxT = yxT_[bl % 2]
            hT3 = hT3_[bl % 2]; hkey = ("hT3", bl % 2)
            if bl + 1 < 4:
                front3(bl + 1)
            outproj(bl - 1, (0,))
            conv_cu(512, 2, hT3, hkey)
            for j in range(2):
                w0 = convw[:, j * 3 + 0:j * 3 + 1]; w1 = convw[:, j * 3 + 1:j * 3 + 2]; w2 = convw[:, j * 3 + 2:j * 3 + 3]
                pg.op("dve", lambda e, j=j, w2=w2: e.tensor_scalar(out=ytap[j], in0=zb[:, j, 2:514], scalar1=w2, scalar2=None, op0=ALU.mult),
                      reads=[("z", j), "gcol"], writes=[("ytap", j)])
                pg.op("dve", lambda e, j=j, w1=w1: e.scalar_tensor_tensor(out=ytap[j], in0=zb[:, j, 1:513], scalar=w1, in1=ytap[j], op0=ALU.mult, op1=ALU.add),
                      reads=[("z", j), ("ytap", j)], writes=[("ytap", j)])
                pg.op("dve", lambda e, j=j, w0=w0: e.scalar_tensor_tensor(out=ytap[j], in0=zb[:, j, 0:512], scalar=w0, in1=ytap[j], op0=ALU.mult, op1=ALU.add),
                      reads=[("z", j), ("ytap", j)], writes=[("ytap", j)])
                pg.op("pool", lambda e, j=j: e.tensor_copy(out=zb[:, j, 0:2], in_=zb[:, j, 512:514]), reads=[("z", j)], writes=[("z", j)])
            for j in range(2):
                pb = next_bank()
                for kc in range(8):
                    pg.op("pe", lambda e, kc=kc, pb=pb, j=j: e.matmul(bank(pb), lhsT=wcx[:, kc, j * 128:(j + 1) * 128], rhs=hT3[:, kc, :],
                                                                    start=(kc == 0), stop=(kc == 7)), reads=[hkey] + WCX, writes=[("ps", pb)])
                pg.op("dve", lambda e, pb=pb, j=j: e.tensor_tensor(out=ytap[j], in0=ytap[j], in1=bank(pb), op=ALU.mult),
                      reads=[("ps", pb), ("ytap", j)], writes=[("ytap", j)])
                pg.op("act", lambda e, j=j: e.activation(out=sqc[j], in_=ytap[j], func=AF.Square), reads=[("ytap", j)], writes=[("sqc", j)])
                pg.op("act", lambda e, j=j: e.activation(out=ycT[:, j, :], in_=ytap[j], func=AF.Copy, scale=gc[:, j:j + 1]),
                      reads=[("ytap", j), "gcol"], writes=[("ycT", bl % 2, j)])
            for ti in range(4):
                for j in range(2):
                    pg.op("pe", lambda e, j=j, ti=ti: e.matmul(psum[:, SSC0 + ti:SSC0 + ti + 1], lhsT=sqc[j][:, ti * 128:(ti + 1) * 128],
                                                              rhs=ones_bf[:, 0:1], start=(j == 0), stop=(j == 1)),
                          reads=[("sqc", j), "ones"], writes=[("ps", 3)], c=0.06)
            pg.op("dve", lambda e, bl=bl: e.tensor_copy(out=ssc[:, bl * 4:(bl + 1) * 4], in_=psum[:, SSC0:SSC0 + 4]), reads=[("ps", 3)], writes=[("ssc", bl)])
            pg.op("act", lambda e, bl=bl: e.activation(out=ncn[:, bl * 4:(bl + 1) * 4], in_=ssc[:, bl * 4:(bl + 1) * 4], func=AF.Sqrt, bias=EPS, scale=1.0 / 256),
                  reads=[("ssc", bl)], writes=[("ncn", bl)])
            pg.op("dve", lambda e, bl=bl: e.reciprocal(out=ncn[:, bl * 4:(bl + 1) * 4], in_=ncn[:, bl * 4:(bl + 1) * 4]), reads=[("ncn", bl)], writes=[("ncn", bl)])
            outproj(bl - 1, (1,))
            for j in range(2):
                pb = next_bank()
                for kc in range(8):
                    pg.op("pe", lambda e, kc=kc, pb=pb, j=j: e.matmul(bank(pb), lhsT=wcx[:, kc, 768 + j * 128:768 + (j + 1) * 128], rhs=hT3[:, kc, :],
                                                                    start=(kc == 0), stop=(kc == 7)), reads=[hkey] + WCX, writes=[("ps", pb)])
                pg.op("act", lambda e, pb=pb, j=j: e.copy(out=xqT[:, j, :], in_=bank(pb)), reads=[("ps", pb)], writes=[("xqT", j)])
            for hx in range(4):
                po = (hx % 2) * 64
                for m in range(2):
                    pb = next_bank()
                    if m == 0:
                        pg.op("pe", lambda e, pb=pb: e.matmul(bank(pb, 1, 0), lhsT=identb, rhs=ones_bf[:, 0:1], start=True, stop=True),
                              reads=["identb", "ones"], writes=[("ps", pb), "rowgrp"], c=0.06)
                    pg.op("pe", lambda e, pb=pb, hx=hx, po=po, m=m: e.matmul(bank(pb), lhsT=kmT[po:po + 64, hx // 2, m * 128:(m + 1) * 128],
                                                                           rhs=xqT[po:po + 64, hx // 2, :], start=True, stop=True),
                          reads=["kmT", ("xqT", hx // 2)], writes=[("ps", pb), "rowgrp"])
                    pg.op("act", lambda e, pb=pb, hx=hx, m=m: e.activation(out=PxT[hx * 2 + m], in_=bank(pb), func=AF.Exp, scale=0.125),
                          reads=[("ps", pb)], writes=[("PxT", hx * 2 + m)])
            outproj(bl - 1, (2,))
            for ti in range(4):
                t = bl * 4 + ti
                for hx in range(4):
                    for m in range(2):
                        pg.op("pe", lambda e, hx=hx, m=m, ti=ti: e.matmul(psum[:, OX0 + hx * 65:OX0 + hx * 65 + 65],
                                                                         lhsT=PxT[hx * 2 + m][:, ti * 128:(ti + 1) * 128], rhs=vm_aug[:, m, hx, 0:65],
                                                                         start=(m == 0), stop=(m == 1)),
                              reads=[("PxT", hx * 2 + m), "vm_aug"], writes=[("ps", 3)], c=0.06)
                ov = psum[:, OX0:OX0 + 260].rearrange("p (a b) -> p a b", a=4)
                ys = t % 2
                pg.op("dve", lambda e, ov=ov: e.reciprocal(out=rzx[:, 0:4], in_=ov[:, :, 64]), reads=[("ps", 3)], writes=["rzx"])
                pg.op("dve", lambda e, ov=ov, ys=ys: e.tensor_tensor(out=yxf[ys], in0=ov[:, :, 0:64], in1=rzx[:, 0:4].unsqueeze(2).broadcast_to([128, 4, 64]), op=ALU.mult),
                      reads=[("ps", 3), "rzx"], writes=[("yxf", ys)])
                yflat = yxf[ys].rearrange("p a b -> p (a b)")
                pg.op("act", lambda e, ys=ys, yflat=yflat, t=t: e.activation(out=yxb[ys], in_=yflat, func=AF.Square, accum_out=ssx[:, t:t + 1]),
                      reads=[("yxf", ys)], writes=[("yxb", ys), ("ssx", t)])
                pg.op("act", lambda e, t=t: e.activation(out=nxn[:, t:t + 1], in_=ssx[:, t:t + 1], func=AF.Sqrt, bias=EPS, scale=1.0 / 256),
                      reads=[("ssx", t)], writes=[("nxn", t)])
                pg.op("dve", lambda e, t=t: e.reciprocal(out=nxn[:, t:t + 1], in_=nxn[:, t:t + 1]), reads=[("nxn", t)], writes=[("nxn", t)])
                pg.op("act", lambda e, ys=ys, yflat=yflat: e.copy(out=yxb[ys], in_=yflat), reads=[("yxf", ys), ("yxb", ys)], writes=[("yxb", ys)])
                for jj in range(2):
                    pg.op("pe", lambda e, jj=jj, ys=ys: e.transpose(out=yxps[:, jj * 128:(jj + 1) * 128], in_=yxb[ys][:, jj * 128:(jj + 1) * 128], identity=identb),
                          reads=[("yxb", ys), "identb"], writes=[("ps", 3)], c=0.09)
                pg.op("dve", lambda e, ti=ti: e.tensor_tensor(out=yxT[:, :, ti * 128:(ti + 1) * 128], in0=yxps.rearrange("p (a b) -> p a b", a=2),
                                                             in1=gx.unsqueeze(2).broadcast_to([128, 2, 128]), op=ALU.mult),
                      reads=[("ps", 3), "gcol"], writes=[("yxT", bl % 2, ti)])
            outproj(bl - 1, (3,))
        outproj(3)
        if "x1" in debug:
            finals.append(dump("x1", x1, [128, 16, 1024], F32))

        if upto == "p3":
            pg.emit(final_waits=finals)
            return nc, dbg_out
        pg.barrier(reorder=False)
        nrot[0] = 3
        A4 = Alloc(20480, ARENA_WORDS)
        h2T = A4([128, 8, 1024], BF16)
        acc = A4([128, 8, 1024], F32)
        sqs4 = A4([128, 1024], BF16)
        xn4 = [A4([128, 1024], BF16) for _ in range(2)]
        wu = [A4([128, 8, 512], BF16) for _ in range(2)]
        wd = [A4([128, 4, 1024], BF16) for _ in range(2)]
        fT = [A4([128, 4, 1024], BF16) for _ in range(2)]
        tmpR = [A4([128, 512], BF16) for _ in range(2)]
        yv4 = [A4([128, 1024], F32) for _ in range(2)]
        sqs_, xn_ = sqs4, xn4
        wdv = w_down.rearrange("(c s p) d -> c p s d", s=4, p=128)
        upc = [0]
        last_store = None
        def front4a(tb_, ti):
            t = tb_ * 8 + ti
            norm_tile3(x1[:, t, :], t % 2, ss4[:, t:t + 1], r4[:, t:t + 1], ("s4", t), [("x1", t)])

        def front4b(tb_, ti):
            t = tb_ * 8 + ti
            transpose_to3(h2T, ti * 128, t % 2, gpremlp, 0, ("h2T", ti))

        def front4(tb_, ti):
            front4a(tb_, ti); front4b(tb_, ti)

        def load_w4(fc):
            ws = fc % 2
            for kc in range(8):
                pg.dma("pool", lambda e, kc=kc, ws=ws, fc=fc: e.dma_start(out=wu[ws][:, kc, :], in_=w_up[kc * 128:(kc + 1) * 128, fc * 512:(fc + 1) * 512]),
                       writes=[("wu", ws, kc)], stream="ldw")
            pg.dma("pool", lambda e, ws=ws, fc=fc: e.dma_start(out=wd[ws], in_=wdv[fc]), writes=[("wd", ws)], stream="ldw")

        load_w4(0)
        for ti in range(4):
            front4(0, ti)
        for tb_ in range(2):
            for fc in range(8):
                ws = fc % 2
                if fc > 0:
                    load_w4(fc)
                WU = [("wu", ws, kc) for kc in range(8)]
                first = (tb_ == 0 and fc == 0)
                sth = [(s_, th_) for th_ in range(2) for s_ in range(4)] if first else [(s_, th_) for s_ in range(4) for th_ in range(2)]
                for (s, th) in sth:
                    if True:
                        H2 = [("h2T", ti) for ti in range(th * 4, th * 4 + 4)]
                        if first and th == 0 and s == 0:
                            front4a(0, 4)
                        pb = next_bank()
                        for kc in range(8):
                            pg.op("pe", lambda e, kc=kc, pb=pb, s=s, th=th, ws=ws: e.matmul(bank(pb), lhsT=wu[ws][:, kc, s * 128:(s + 1) * 128],
                                                                                          rhs=h2T[:, kc, th * 512:(th + 1) * 512], start=(kc == 0), stop=(kc == 7)),
                                  reads=WU + H2, writes=[("ps", pb)])
                        rs = upc[0] % 2; upc[0] += 1
                        pg.op("act", lambda e, pb=pb, rs=rs: e.activation(out=tmpR[rs], in_=bank(pb), func=AF.Relu), reads=[("ps", pb)], writes=[("tmpR", rs)])
                        pg.op("dve", lambda e, rs=rs, s=s, th=th, ws=ws: e.tensor_tensor(out=fT[ws][:, s, th * 512:(th + 1) * 512], in0=tmpR[rs], in1=tmpR[rs], op=ALU.mult),
                              reads=[("tmpR", rs)], writes=[("fT", ws, s, th)])
                        if first and th == 0:
                            front4b(0, 4 + s)
                            if s < 3:
                                front4a(0, 5 + s)
                nxt = (fc == 7 and tb_ == 0)
                if nxt:
                    load_w4(0)
                    front4a(1, 0)
                for ti in range(8):
                    t = tb_ * 8 + ti
                    if nxt and ti >= 1:
                        front4b(1, ti - 1)
                        front4a(1, ti)
                    pr = opair[0] % 2; opair[0] += 1
                    pbase = OP0 + pr * 1024
                    for nh in range(2):
                        for s in range(4):
                            pg.op("pe", lambda e, s=s, nh=nh, ti=ti, ws=ws, pbase=pbase: e.matmul(
                                psum[:, pbase + nh * 512:pbase + (nh + 1) * 512], lhsT=fT[ws][:, s, ti * 128:(ti + 1) * 128],
                                rhs=wd[ws][:, s, nh * 512:(nh + 1) * 512], start=(s == 0), stop=(s == 3)),
                                reads=[("fT", ws, s, ti // 4), ("wd", ws)], writes=[("psP", pr)])
                    pv = psum[:, pbase:pbase + 1024]
                    if fc == 0:
                        pg.op("act", lambda e, pv=pv, ti=ti: e.copy(out=acc[:, ti, :], in_=pv), reads=[("psP", pr)], writes=[("acc4", ti)])
                    elif fc < 7:
                        pg.op("dve", lambda e, pv=pv, ti=ti: e.tensor_tensor(out=acc[:, ti, :], in0=acc[:, ti, :], in1=pv, op=ALU.add),
                              reads=[("psP", pr), ("acc4", ti)], writes=[("acc4", ti)])
                    else:
                        ysl = t % 2
                        pg.op("dve", lambda e, pv=pv, ti=ti, ysl=ysl: e.tensor_tensor(out=yv4[ysl], in0=acc[:, ti, :], in1=pv, op=ALU.add),
                              reads=[("psP", pr), ("acc4", ti)], writes=[("yv4", ysl)])
                        pg.op("act", lambda e, ysl=ysl, t=t: e.activation(out=sqs_, in_=yv4[ysl], func=AF.Square, accum_out=ssf[:, t:t + 1]),
                              reads=[("yv4", ysl)], writes=["sqs3", ("ssf", t)])
                        pg.op("act", lambda e, t=t: e.activation(out=rf[:, t:t + 1], in_=ssf[:, t:t + 1], func=AF.Sqrt, bias=EPS, scale=1.0 / D),
                              reads=[("ssf", t)], writes=[("rf", t)])
                        pg.op("dve", lambda e, t=t: e.reciprocal(out=rf[:, t:t + 1], in_=rf[:, t:t + 1]), reads=[("rf", t)], writes=[("rf", t)])
                        pg.op("dve", lambda e, ysl=ysl, t=t: e.scalar_tensor_tensor(out=yv4[ysl], in0=yv4[ysl], scalar=rf[:, t:t + 1], in1=Gpl, op0=ALU.mult, op1=ALU.mult),
                              reads=[("yv4", ysl), ("rf", t), "Gpl"], writes=[("yv4", ysl)])
                        pg.op("pool", lambda e, ysl=ysl, t=t: e.tensor_tensor(out=yv4[ysl], in0=yv4[ysl], in1=x1[:, t, :], op=ALU.add),
                              reads=[("yv4", ysl), ("x1", t)], writes=[("yv4", ysl)])
                        last_store = pg.dma("sync", lambda e, ysl=ysl, t=t: e.dma_start(out=outd[t * 128:(t + 1) * 128, :], in_=yv4[ysl]),
                                            reads=[("yv4", ysl)], stream="st")
                if nxt:
                    front4b(1, 7)
        finals.append(last_store)
        pg.emit(final_waits=finals)
    return nc, dbg_out


_PROG_CACHE = {}


def _host_constants():
    ident = np.eye(128, dtype=np.float32)
    j = np.arange(128)[:, None]; i = np.arange(128)[None, :]
    mask2 = np.concatenate([(j >= i).astype(np.float32), (j <= i).astype(np.float32)], axis=1)
    half = 32
    invf = (np.float32(10000.0) ** (-(np.arange(half, dtype=np.float32) * np.float32(2.0) / np.float32(64)))).astype(np.float32)
    invf = np.tile(invf[None, :], (128, 1)).astype(np.float32)
    E = np.zeros((8, 4, 128), np.float32)
    for h in range(8):
        E[h, h // 2, (h % 2) * 64:(h % 2) * 64 + 64] = 1.0
    return ident, mask2, invf, E.reshape(8, 512)


def make_in_maps(x, mem, positions, g_pre_mix, g_mem, w_in, w_mem_kv, conv_w, g_attn_out, g_conv_out,
                 g_xattn_out, w_out, g_post_mix, g_pre_mlp, w_up, w_down, g_post_mlp):
    f = lambda a: np.ascontiguousarray(np.asarray(a))
    x = f(x); mem = f(mem); positions = f(positions)
    ident, mask2, invf, E = _host_constants()
    gcol = np.zeros((128, 40), np.float32)
    gcol[:, 0:8] = f(g_pre_mix)[0].reshape(8, 128).T
    gcol[:, 8:16] = f(g_mem)[0].reshape(8, 128).T
    gcol[:, 16:24] = f(g_pre_mlp)[0].reshape(8, 128).T
    gcol[:, 24:28] = f(g_attn_out)[0].reshape(4, 128).T
    gcol[:, 28:30] = f(g_conv_out)[0].reshape(2, 128).T
    gcol[:, 30:32] = f(g_xattn_out)[0].reshape(2, 128).T
    cw = f(conv_w)[0]
    for j in range(2):
        for tap in range(3):
            gcol[:, 32 + j * 3 + tap] = cw[tap, j * 128:(j + 1) * 128]
    shared = dict(w_in=f(w_in)[0], w_mem_kv=f(w_mem_kv)[0], w_out=f(w_out)[0], w_up=f(w_up)[0], w_down=f(w_down)[0],
                  gcol=gcol, g_post_mix=f(g_post_mix)[0], g_post_mlp=f(g_post_mlp)[0],
                  ident=ident, mask2=mask2, invf=invf, Esel=E)
    in_maps = []
    for c in range(8):
        b, half = c // 2, c % 2
        xo = x[b, half * NOWN:(half + 1) * NOWN]
        if half == 1:
            xh = x[b, 0:NOWN]
            pos = positions[b, 0:SEQ]
            hbv = 0.0
        else:
            xh = np.zeros((NOWN, D), np.float32)
            pos = np.concatenate([np.zeros(NOWN, np.int32), positions[b, 0:NOWN]])
            hbv = -30000.0
        posT = np.ascontiguousarray(pos.reshape(32, 128).T.astype(np.int32))
        m = dict(shared)
        m.update(xo=np.ascontiguousarray(xo), xh=np.ascontiguousarray(xh), mem=np.ascontiguousarray(mem[b]),
                 posT=posT, hb=np.full((128, 1), hbv, np.float32))
        in_maps.append(m)
    return in_maps


def kernel(**inputs):
    in_maps = make_in_maps(**inputs)
    if "nc" not in _PROG_CACHE:
        _PROG_CACHE["nc"] = build_program()[0]
    nc = _PROG_CACHE["nc"]
    res = run_bass_kernel_spmd(nc, in_maps, core_ids=list(range(8)))
    out = np.zeros((NB, SEQ, D), np.float32)
    for c in range(8):
        b, half = c // 2, c % 2
        out[b, half * NOWN:(half + 1) * NOWN] = res.results[c]["out"]
    return out
```

```python
import contextlib
import types
import numpy as np
import concourse.bass as bass
import concourse.mybir as mybir
from concourse.bass_utils import run_bass_kernel_spmd

F32 = mybir.dt.float32
BF16 = mybir.dt.bfloat16
I32 = mybir.dt.int32
AF = mybir.ActivationFunctionType
ALU = mybir.AluOpType

D = 1024
SEQ = 4096
NB = 4
NOWN = 2048
NT = 16
DFF = 4096
EPS = 1e-6
PI = float(np.pi)
TWO_PI = float(2 * np.pi)
ARENA_WORDS = 51200
REORDER = True
ALLSYNC = True


def _freeze(fn):
    if fn.__closure__ is None:
        return fn
    cells = []
    for c in fn.__closure__:
        try:
            cells.append(types.CellType(c.cell_contents))
        except ValueError:
            cells.append(c)
    return types.FunctionType(fn.__code__, fn.__globals__, fn.__name__, fn.__defaults__, tuple(cells))


class Prog:
    ENG = {"sync": "sync", "act": "scalar", "pool": "gpsimd", "pe": "tensor", "dve": "vector"}
    DEF_COST = {"sync": 0.05, "act": 0.6, "pool": 0.9, "pe": 0.25, "dve": 0.5}
    WINDOW = {"sync": 8, "act": 48, "pool": 48, "pe": 192, "dve": 48}

    def __init__(self, nc, reorder=True):
        self.nc = nc
        self.ops = []
        self.last_w = {}
        self.readers = {}
        self.phase = 0
        self.noreorder = set()
        self.reorder = reorder

    def barrier(self, reorder=True):
        self.phase += 1
        if not reorder:
            self.noreorder.add(self.phase)

    def _add(self, eng, fn, reads, writes, dma_stream=None, extra=(), c=None):
        deps = {}
        for d in extra:
            deps[d] = "raw"
        for k in reads:
            if k in self.last_w:
                deps[self.last_w[k]] = "raw"
        for k in writes:
            if k in self.last_w:
                deps.setdefault(self.last_w[k], "waw")
            for r in self.readers.get(k, ()):
                deps.setdefault(r, "war")
        idx = len(self.ops)
        deps.pop(idx, None)
        if c is None:
            c = self.DEF_COST[eng]
        self.ops.append(dict(eng=eng, fn=_freeze(fn), deps=deps, dma=dma_stream, phase=self.phase, c=c))
        for k in writes:
            self.last_w[k] = idx
            self.readers[k] = []
        for k in reads:
            self.readers.setdefault(k, []).append(idx)
        return idx

    def op(self, eng, fn, reads=(), writes=(), extra=(), c=None):
        return self._add(eng, fn, reads, writes, None, extra, c)

    def dma(self, eng, fn, reads=(), writes=(), stream="d0", extra=(), c=None):
        return self._add(eng, fn, reads, writes, stream, extra, 2.5 if c is None else c)

    def _schedule(self):
        ops = self.ops
        nph = self.phase + 1
        per = {e: [[] for _ in range(nph)] for e in self.ENG}
        for i, o in enumerate(ops):
            per[o["eng"]][o["phase"]].append(i)
        order = {e: [] for e in self.ENG}
        phase_tokens = []
        done = {}
        t_eng = {e: 0.0 for e in self.ENG}
        for ph in range(nph):
            queues = {e: list(per[e][ph]) for e in self.ENG}
            tstart = max(t_eng.values())
            for e in self.ENG:
                t_eng[e] = max(t_eng[e], tstart)
            remaining = sum(len(q) for q in queues.values())
            while remaining:
                best = None
                for e, q in queues.items():
                    if not q:
                        continue
                    w = self.WINDOW[e] if (self.reorder and ph not in self.noreorder) else 1
                    dma_blocked = False
                    for pos in range(min(w, len(q))):
                        i = q[pos]
                        o = ops[i]
                        if o["dma"] is not None:
                            if dma_blocked:
                                continue
                            dma_blocked = True
                        ready = tstart
                        ok = True
                        for d in o["deps"]:
                            if ops[d]["phase"] < ph:
                                continue
                            t = done.get(d)
                            if t is None:
                                ok = False
                                break
                            if t > ready:
                                ready = t
                        if not ok:
                            continue
                        st = max(ready, t_eng[e])
                        cand = (st, i, e, pos)
                        if best is None or cand < best:
                            best = cand
                        if st <= t_eng[e]:
                            break
                assert best is not None, "scheduler deadlock"
                st, i, e, pos = best
                o = ops[i]
                queues[e].pop(pos)
                order[e].append(i)
                if o["dma"] is not None:
                    t_eng[e] = st + (1.0 if e == "pool" else 0.1)
                    done[i] = st + o["c"]
                else:
                    t_eng[e] = st + o["c"]
                    done[i] = st + o["c"] + 0.1
                remaining -= 1
            toks = []
            for e in self.ENG:
                comp = [i for i in order[e] if ops[i]["phase"] == ph and ops[i]["dma"] is None]
                if comp:
                    toks.append(comp[-1])
            toks += [i for i in per_all_dma(ops, ph)]
            phase_tokens.append(toks)
            t_eng = {e: max(t_eng.values()) for e in self.ENG}
        self.sim_time = max(t_eng.values())
        return order, phase_tokens

    def emit(self, final_waits=()):
        nc = self.nc
        ops = self.ops
        order, phase_tokens = self._schedule()
        pool = {"ldw": 16}
        cnt = {}
        val = {}
        dma_prev = {}
        nstream = {}
        last_on = {}
        issue_pos = {}
        for e in self.ENG:
            for n, i in enumerate(order[e]):
                issue_pos[i] = n
        for e in self.ENG:
            for i in order[e]:
                o = ops[i]
                if o["dma"] is None:
                    continue
                n = nstream.get(o["dma"], 0)
                nstream[o["dma"]] = n + 1
                key = ("dma", "%s%d" % (o["dma"], n % pool.get(o["dma"], 4)))
                if key in last_on:
                    dma_prev[i] = last_on[key]
                last_on[key] = i
                cnt[key] = cnt.get(key, 0) + 16
                val[i] = (key, cnt[key])

        def needs_wait(i, d):
            o, p = ops[i], ops[d]
            if p["dma"] is None and o["dma"] is None and p["eng"] == o["eng"]:
                if p["eng"] == "pe":
                    return False
                return ALLSYNC or o["deps"].get(d) == "raw"
            return True

        waited = set()
        for i, o in enumerate(ops):
            for d in o["deps"]:
                if needs_wait(i, d):
                    waited.add(d)
        for toks in phase_tokens:
            waited.update(toks)
        waited.update(final_waits)
        for e in self.ENG:
            k = ("eng", e)
            for i in order[e]:
                if ops[i]["dma"] is None and i in waited:
                    cnt[k] = cnt.get(k, 0) + 1
                    val[i] = (k, cnt[k])
        keys = sorted(set(k for k, _ in val.values()))
        with contextlib.ExitStack() as es:
            sems = {}
            for k in keys:
                sems[k] = es.enter_context(nc.semaphore("s_%s_%s" % k))
            block = es.enter_context(nc.Block())

            def make(engname):
                idxs = order[engname]

                def body(e):
                    seen = {}

                    def wait(d):
                        if d not in val:
                            return
                        k, v = val[d]
                        if seen.get(k, 0) >= v:
                            return
                        seen[k] = v
                        e.wait_ge(sems[k], v)
                    for i in idxs:
                        o = ops[i]
                        for ph in range(o["phase"]):
                            for d in phase_tokens[ph]:
                                if ops[d]["eng"] != engname or ops[d]["dma"] is not None or o["dma"] is not None:
                                    wait(d)
                        for d in o["deps"]:
                            if needs_wait(i, d):
                                wait(d)
                        if i in dma_prev:
                            wait(dma_prev[i])
                        ins = o["fn"](e)
                        if i in val:
                            k, v = val[i]
                            ins.then_inc(sems[k], 16 if o["dma"] is not None else 1)
                    if engname == "sync":
                        for d in final_waits:
                            wait(d)
                return body

            for engname, attr in self.ENG.items():
                if order[engname] or engname == "sync":
                    getattr(block, attr)(make(engname))


def per_all_dma(ops, ph):
    return [i for i, o in enumerate(ops) if o["phase"] == ph and o["dma"] is not None]


def build_program(debug=(), upto=None):
    nc = bass.Bass("TRN2", target_bir_lowering=False)
    dram_in = {}

    def din(name, shape, dt=F32):
        dram_in[name] = nc.dram_tensor(name, list(shape), dt, kind="ExternalInput").ap()
        return dram_in[name]

    xo = din("xo", [NOWN, D]); xh = din("xh", [NOWN, D]); memd = din("mem", [256, D])
    posT = din("posT", [128, 32], I32); hbd = din("hb", [128, 1])
    w_in = din("w_in", [D, 2560]); w_kv = din("w_mem_kv", [D, 512]); w_out = din("w_out", [D, D])
    w_up = din("w_up", [D, DFF]); w_down = din("w_down", [DFF, D])
    gcold = din("gcol", [128, 40])
    gpmd = din("g_post_mix", [D]); gpld = din("g_post_mlp", [D])
    identd = din("ident", [128, 128]); maskd = din("mask2", [128, 256]); invfd = din("invf", [128, 32])
    Ed = din("Esel", [8, 512])
    outd = nc.dram_tensor("out", [NOWN, D], F32, kind="ExternalOutput").ap()
    dbg_out = {}

    with contextlib.ExitStack() as es:
        arena = es.enter_context(nc.sbuf_tensor("arena", [128, ARENA_WORDS], F32))
        psum = es.enter_context(nc.psum_tensor("psum", [128, 4096], F32))

        def view(off, shape, dt):
            n = int(np.prod(shape[1:]))
            nb = n * (2 if dt == BF16 else 4)
            assert nb % 4 == 0
            nw = nb // 4
            assert off + nw <= ARENA_WORDS, (off, nw)
            a = arena[0:shape[0], off:off + nw]
            if dt != F32:
                a = a.bitcast(dt)
            if len(shape) == 3:
                a = a.rearrange("p (a b) -> p a b", a=shape[1])
            elif len(shape) == 4:
                a = a.rearrange("p (a b c) -> p a b c", a=shape[1], b=shape[2])
            return a, off + nw

        class Alloc:
            def __init__(self, start, end):
                self.off = start; self.end = end

            def __call__(self, shape, dt):
                a, self.off = view(self.off, shape, dt)
                assert self.off <= self.end, (self.off, self.end)
                return a

        def bank(b, n=512, off=0):
            return psum[:, b * 512 + off:b * 512 + off + n]

        def bank_bf(b):
            return psum[:, b * 512:(b + 1) * 512].bitcast(BF16)

        pg = Prog(nc, reorder=REORDER)

        A0 = Alloc(0, 2048)
        identb = A0([128, 128], BF16); identf = A0([128, 128], F32)
        mask2 = A0([128, 2, 128], BF16); invf = A0([128, 32], F32)
        ones_bf = A0([128, 8], BF16); Esel = A0([128, 4, 128], BF16)
        gcol = A0([128, 40], F32); hb = A0([128, 2], F32)
        posi = A0([128, 32], I32); posf = A0([128, 32], F32)
        ss1 = A0([128, 34], F32); r1 = A0([128, 34], F32)
        ss3 = A0([128, 18], F32); r3 = A0([128, 18], F32)
        ss4 = A0([128, 16], F32); r4 = A0([128, 16], F32)
        ssa = A0([128, 16], F32); na = A0([128, 16], F32)
        ssc = A0([128, 16], F32); ncn = A0([128, 16], F32)
        ssx = A0([128, 16], F32); nxn = A0([128, 16], F32)
        ssy = A0([128, 16], F32); ry = A0([128, 16], F32)
        ssf = A0([128, 16], F32); rf = A0([128, 16], F32)
        rzx = A0([128, 8], F32)
        Zs = A0([128, 2, 8], F32)
        Zh = A0([128, 2, 16], BF16)
        kmT = A0([128, 2, 256], BF16); vm_aug = A0([128, 2, 4, 80], BF16)
        gpre = gcol[:, 0:8]; gmem = gcol[:, 8:16]; gpremlp = gcol[:, 16:24]
        ga = gcol[:, 24:28]; gc = gcol[:, 28:30]; gx = gcol[:, 30:32]; convw = gcol[:, 32:40]
        cosT, o_ = view(2048, [128, 32, 32], F32)
        sinT, o_ = view(o_, [128, 32, 32], F32)
        Gpm, o_ = view(2048, [128, 1024], F32)
        Gpl, o_ = view(o_, [128, 1024], F32)

        A1 = Alloc(4096, ARENA_WORDS)
        qT = A1([128, 4, 2048], BF16); kT = A1([128, 4, 4096], BF16); vT = A1([128, 4, 4096], BF16)
        assert A1.off == 24576
        wqkv = A1([128, 8, 1536], BF16)
        assert A1.off == 30720
        xt = [A1([128, 1024], F32) for _ in range(3)]
        sqs = A1([128, 1024], BF16)
        xn = [A1([128, 1024], BF16) for _ in range(4)]
        hT = [A1([128, 8, 512], BF16) for _ in range(2)]
        qk_sb = [A1([128, 8, 2, 32], F32) for _ in range(4)]
        rtmp = [A1([128, 8, 32], F32) for _ in range(8)]
        qr = [A1([128, 8, 2, 32], BF16) for _ in range(4)]
        wkv = A1([128, 8, 512], BF16)
        memT = A1([128, 8, 256], BF16)

        dcount = [0]

        def load_small(dst, src, eng="sync", stream="ld", key=None):
            key = key or ("c", dcount[0]); dcount[0] += 1
            pg.dma(eng, lambda e: e.dma_start(out=dst, in_=src), writes=[key], stream=stream)
            return key

        def norm_tile(src_ap, slot, ss_col, r_col, skey, xkey, Dn, srckeys):
            pg.op("act", lambda e: e.activation(out=sqs, in_=src_ap, func=AF.Square, accum_out=ss_col),
                  reads=srckeys, writes=["sqs", skey], c=1.0)
            pg.op("act", lambda e: e.activation(out=r_col, in_=ss_col, func=AF.Sqrt, bias=EPS, scale=1.0 / Dn),
                  reads=[skey], writes=[skey + ("r",)], c=0.25)
            pg.op("dve", lambda e: e.reciprocal(out=r_col, in_=r_col), reads=[skey + ("r",)], writes=[skey + ("r",)], c=0.15)
            pg.op("act", lambda e: e.activation(out=xn[slot], in_=src_ap, func=AF.Copy, scale=r_col),
                  reads=srckeys + [skey + ("r",)], writes=[("xn", slot)], c=1.0)

        def transpose_to(dstT, col0, slot, gcols_ap, tbank, dkey):
            pb = bank_bf(tbank)
            for kc in range(8):
                pg.op("pe", lambda e, kc=kc: e.transpose(out=pb[:, kc * 128:(kc + 1) * 128],
                                                        in_=xn[slot][:, kc * 128:(kc + 1) * 128], identity=identb),
                      reads=[("xn", slot), "identb"], writes=[("ps", tbank)], c=0.09)
            pg.op("dve", lambda e: e.tensor_tensor(out=dstT[:, :, col0:col0 + 128],
                                                   in0=pb.rearrange("p (a b) -> p a b", a=8),
                                                   in1=gcols_ap.unsqueeze(2).broadcast_to([128, 8, 128]), op=ALU.mult),
                  reads=[("ps", tbank), "gcol"], writes=[dkey], c=1.2)

        def dump(name, ap, shape, dt):
            d = nc.dram_tensor("dbg_" + name, list(shape), dt, kind="ExternalOutput").ap()
            dbg_out[name] = d
            pg.barrier()
            return pg.dma("sync", lambda e: e.dma_start(out=d, in_=ap), stream="st")
        finals = []
        pg.dma("pool", lambda e: e.dma_start(out=identb, in_=identd[:, :]), writes=["identb"], stream="ldw")
        pg.dma("pool", lambda e: e.dma_start(out=mask2.rearrange("p a b -> p (a b)"), in_=maskd[:, :]), writes=["mask2"], stream="ldw")
        pg.dma("sync", lambda e: e.dma_start(out=identf, in_=identd[:, :]), writes=["identf"], stream="ld")
        pg.dma("sync", lambda e: e.dma_start(out=invf, in_=invfd[:, :]), writes=["invf"], stream="ld")
        pg.dma("sync", lambda e: e.dma_start(out=gcol, in_=gcold[:, :]), writes=["gcol"], stream="ld")
        pg.dma("sync", lambda e: e.dma_start(out=hb[:, 0:1], in_=hbd[:, :]), writes=["hb"], stream="ld")
        pg.dma("sync", lambda e: e.dma_start(out=posi, in_=posT[:, :]), writes=["posi"], stream="ld")
        pg.dma("pool", lambda e: e.dma_start(out=Esel[0:8].rearrange("p a b -> p (a b)"), in_=Ed[:, :]), writes=["Esel"], stream="ldw")
        pg.op("dve", lambda e: e.memset(ones_bf, 1.0), writes=["ones"])
        pg.op("dve", lambda e: e.memset(vm_aug.rearrange("p a b c -> p (a b c)"), 1.0), writes=["vm_aug"])
        for kc in range(8):
            pg.dma("pool", lambda e, kc=kc: e.dma_start(out=wkv[:, kc, :], in_=w_kv[kc * 128:(kc + 1) * 128, :]),
                   writes=[("wkv", kc)], stream="ldw")
        for kc in range(8):
            pg.dma("pool", lambda e, kc=kc: e.dma_start(out=wqkv[:, kc, :], in_=w_in[kc * 128:(kc + 1) * 128, 0:1536]),
                   writes=[("wqkv", kc)], stream="ldw")
        WQKV = [("wqkv", kc) for kc in range(8)]
        WKV = [("wkv", kc) for kc in range(8)]

        t_off = 30720 + 3072 + 512 + 2048
        angv, t_o = view(t_off, [128, 32, 32], F32)
        kfv, t_o = view(t_o, [128, 32, 32], F32)
        kiv, t_o = view(t_o, [128, 32, 32], I32)
        mmv, t_o = view(t_o, [128, 32, 32], F32)
        TK = [("hT", 0), ("hT", 1)]
        pg.op("dve", lambda e: e.tensor_copy(out=posf, in_=posi), reads=["posi"], writes=["posf"])
        pg.op("dve", lambda e: e.tensor_tensor(out=angv, in0=posf.unsqueeze(2).broadcast_to([128, 32, 32]),
                                               in1=invf.unsqueeze(1).broadcast_to([128, 32, 32]), op=ALU.mult),
              reads=["posf", "invf"], writes=TK)

        def range_reduce_sin(dst, shift):
            pg.op("dve", lambda e: e.tensor_scalar(out=mmv, in0=angv, scalar1=shift, scalar2=None, op0=ALU.add), reads=TK, writes=TK)
            pg.op("dve", lambda e: e.tensor_scalar(out=kfv, in0=mmv, scalar1=1.0 / TWO_PI, scalar2=None, op0=ALU.mult), reads=TK, writes=TK)
            pg.op("dve", lambda e: e.tensor_copy(out=kiv, in_=kfv), reads=TK, writes=TK)
            pg.op("dve", lambda e: e.tensor_copy(out=kfv, in_=kiv), reads=TK, writes=TK)
            pg.op("dve", lambda e: e.scalar_tensor_tensor(out=mmv, in0=kfv, scalar=-TWO_PI, in1=mmv, op0=ALU.mult, op1=ALU.add), reads=TK, writes=TK)
            pg.op("dve", lambda e: e.tensor_single_scalar(out=kfv, in_=mmv, scalar=PI, op=ALU.is_gt), reads=TK, writes=TK)
            pg.op("dve", lambda e: e.scalar_tensor_tensor(out=mmv, in0=kfv, scalar=-TWO_PI, in1=mmv, op0=ALU.mult, op1=ALU.add), reads=TK, writes=TK)
            pg.op("dve", lambda e: e.tensor_single_scalar(out=kfv, in_=mmv, scalar=-PI, op=ALU.is_lt), reads=TK, writes=TK)
            pg.op("dve", lambda e: e.scalar_tensor_tensor(out=mmv, in0=kfv, scalar=TWO_PI, in1=mmv, op0=ALU.mult, op1=ALU.add), reads=TK, writes=TK)
            pg.op("act", lambda e: e.activation(out=dst, in_=mmv, func=AF.Sin), reads=TK, writes=TK + ["tables"])

        range_reduce_sin(sinT, 0.0)
        range_reduce_sin(cosT, PI / 2)

        xcount = [0]

        def load_x(src_ap):
            slot = xcount[0] % len(xt); xcount[0] += 1
            pg.dma("sync", lambda e: e.dma_start(out=xt[slot], in_=src_ap), writes=[("xt", slot)], stream="ldx")
            return slot

        for m in range(2):
            s = load_x(memd[m * 128:(m + 1) * 128, :])
            norm_tile(xt[s], m, ss1[:, 32 + m:33 + m], r1[:, 32 + m:33 + m], ("s1", 32 + m), None, D, [("xt", s)])
            transpose_to(memT, m * 128, m, gmem, 0, ("memT", m))
        MEMT = [("memT", 0), ("memT", 1)]
        for c in range(2):
            for kc in range(8):
                pg.op("pe", lambda e, c=c, kc=kc: e.matmul(bank(1, 256), lhsT=wkv[:, kc, c * 128:(c + 1) * 128], rhs=memT[:, kc, :],
                                                           start=(kc == 0), stop=(kc == 7)),
                      reads=WKV + MEMT, writes=[("ps", 1)])
            pg.op("act", lambda e, c=c: e.copy(out=kmT[:, c, :], in_=bank(1, 256)), reads=[("ps", 1)], writes=["kmT"])
        for m in range(2):
            for kc in range(8):
                pg.op("pe", lambda e, m=m, kc=kc: e.matmul(bank(2, 256), lhsT=memT[:, kc, m * 128:(m + 1) * 128], rhs=wkv[:, kc, 256:512],
                                                           start=(kc == 0), stop=(kc == 7)),
                      reads=WKV + MEMT, writes=[("ps", 2)])
            pg.op("act", lambda e, m=m: e.copy(out=vm_aug[:, m, :, 0:64], in_=bank(2, 256).rearrange("p (a b) -> p a b", a=4)),
                  reads=[("ps", 2)], writes=["vm_aug"])

        if upto == "p0":
            f1 = dump("kmT", kmT, [128, 2, 256], BF16); f2 = dump("vm", vm_aug.rearrange("p a b c -> p (a b c)"), [128, 640], BF16)
            f3 = dump("cos", cosT, [128, 32, 32], F32); f4 = dump("sin", sinT, [128, 32, 32], F32)
            pg.emit(final_waits=[f1, f2, f3, f4])
            return nc, dbg_out
        mmb = [0]
        nrot = [3]

        def next_bank():
            b = 1 + (mmb[0] % nrot[0]); mmb[0] += 1
            return b
        trb = [0]

        def rope_and_store(pbank, gt, dstT, col0, slotk):
            sl = slotk % 4
            rs = (slotk % 2) * 4
            pg.op("act", lambda e: e.copy(out=qk_sb[sl].rearrange("p a b c -> p (a b c)"), in_=bank(pbank)),
                  reads=[("ps", pbank)], writes=[("qk_sb", sl)])
            cs = cosT[:, gt, :].unsqueeze(1).broadcast_to([128, 8, 32])
            sn = sinT[:, gt, :].unsqueeze(1).broadcast_to([128, 8, 32])
            t1 = qk_sb[sl][:, :, 0, :]; t2 = qk_sb[sl][:, :, 1, :]
            pg.op("dve", lambda e: e.tensor_tensor(out=rtmp[rs + 0], in0=t1, in1=cs, op=ALU.mult), reads=[("qk_sb", sl), "tables"], writes=[("rt", rs + 0)], c=0.4)
            pg.op("dve", lambda e: e.tensor_tensor(out=rtmp[rs + 1], in0=t2, in1=sn, op=ALU.mult), reads=[("qk_sb", sl), "tables"], writes=[("rt", rs + 1)], c=0.4)
            pg.op("dve", lambda e: e.tensor_tensor(out=qr[sl][:, :, 0, :], in0=rtmp[rs + 0], in1=rtmp[rs + 1], op=ALU.subtract),
                  reads=[("rt", rs + 0), ("rt", rs + 1)], writes=[("qr", sl, 0)], c=0.4)
            pg.op("pool", lambda e: e.tensor_tensor(out=rtmp[rs + 2], in0=t1, in1=sn, op=ALU.mult), reads=[("qk_sb", sl), "tables"], writes=[("rt", rs + 2)])
            pg.op("pool", lambda e: e.tensor_tensor(out=rtmp[rs + 3], in0=t2, in1=cs, op=ALU.mult), reads=[("qk_sb", sl), "tables"], writes=[("rt", rs + 3)])
            pg.op("pool", lambda e: e.tensor_tensor(out=qr[sl][:, :, 1, :], in0=rtmp[rs + 2], in1=rtmp[rs + 3], op=ALU.add),
                  reads=[("rt", rs + 2), ("rt", rs + 3)], writes=[("qr", sl, 1)])
            tb = 4 + (trb[0] % 2); trb[0] += 1
            pb = bank_bf(tb)
            qflat = qr[sl].rearrange("p a b c -> p (a b c)")
            for c in range(4):
                pg.op("pe", lambda e, c=c: e.transpose(out=pb[:, c * 128:(c + 1) * 128], in_=qflat[:, c * 128:(c + 1) * 128], identity=identb),
                      reads=[("qr", sl, 0), ("qr", sl, 1), "identb"], writes=[("ps", tb)], c=0.09)
            pg.op("act", lambda e: e.copy(out=dstT[:, :, col0:col0 + 128], in_=pb[:, 0:512].rearrange("p (a b) -> p a b", a=4)),
                  reads=[("ps", tb)], writes=[("qkT", gt, id(dstT))])

        ropec = [0]
        def front1(nb):
            own = nb >= 4
            hs = nb % 2
            for ti in range(4):
                gt = nb * 4 + ti
                src = xo[(gt - 16) * 128:(gt - 15) * 128, :] if own else xh[gt * 128:(gt + 1) * 128, :]
                s = load_x(src)
                norm_tile(xt[s], gt % 4, ss1[:, gt:gt + 1], r1[:, gt:gt + 1], ("s1", gt), None, D, [("xt", s)])
                transpose_to(hT[hs], ti * 128, gt % 4, gpre, 0, ("hT", hs))

        front1(0)
        for nb in range(8):
            own = nb >= 4
            hs = nb % 2
            if nb + 1 < 8:
                front1(nb + 1)
            HK = [("hT", hs)]
            for ti in range(4):
                gt = nb * 4 + ti
                for which in ([1, 0] if own else [1]):
                    pb = next_bank()
                    for kc in range(8):
                        pg.op("pe", lambda e, kc=kc, pb=pb, which=which, ti=ti: e.matmul(
                            bank(pb), lhsT=hT[hs][:, kc, ti * 128:(ti + 1) * 128], rhs=wqkv[:, kc, which * 512:(which + 1) * 512],
                            start=(kc == 0), stop=(kc == 7)), reads=HK + WQKV, writes=[("ps", pb)])
                    if which == 1:
                        rope_and_store(pb, gt, kT, gt * 128, ropec[0])
                    else:
                        rope_and_store(pb, gt, qT, (gt - 16) * 128, ropec[0])
                    ropec[0] += 1
            for j in range(4):
                pb = next_bank()
                for kc in range(8):
                    pg.op("pe", lambda e, kc=kc, pb=pb, j=j: e.matmul(
                        bank(pb), lhsT=wqkv[:, kc, 1024 + j * 128:1024 + (j + 1) * 128], rhs=hT[hs][:, kc, :],
                        start=(kc == 0), stop=(kc == 7)), reads=HK + WQKV, writes=[("ps", pb)])
                pg.op("act" if j % 2 == 0 else "dve",
                      (lambda e, pb=pb, j=j: e.copy(out=vT[:, j, nb * 512:(nb + 1) * 512], in_=bank(pb))) if j % 2 == 0 else
                      (lambda e, pb=pb, j=j: e.tensor_copy(out=vT[:, j, nb * 512:(nb + 1) * 512], in_=bank(pb))),
                      reads=[("ps", pb)], writes=[("vT", nb, j)])

        if "qkv" in debug:
            finals.append(dump("qT", qT, [128, 4, 2048], BF16))
            finals.append(dump("kT", kT, [128, 4, 4096], BF16))
            finals.append(dump("vT", vT, [128, 4, 4096], BF16))

        if upto == "p1":
            pg.emit(final_waits=finals)
            return nc, dbg_out
        pg.barrier()
        A2 = Alloc(24576, ARENA_WORDS)
        ya_acc = A2([128, 4, 2048], F32)
        ZT_acc = A2([128, 2048], F32)
        PT = [A2([128, 4, 128], BF16) for _ in range(8)]
        Vaug = [A2([128, 8, 80], BF16) for _ in range(3)]
        Obf = [A2([128, 8, 64], BF16) for _ in range(2)]
        AT = Alloc(2048, 4096)
        rZ = AT([128, 512], F32)
        sqA = [AT([128, 512], BF16) for _ in range(4)]
        rZh = AT([128, 512], BF16); rZl = AT([128, 512], BF16)
        assert A2.off <= 38400, A2.off
        OB = {0: 7 * 512, 1: 0}
        SSA0 = 300
        A2.off = 38400
        ya_T = A2([128, 4, 2048], BF16)
        wcx = A2([128, 8, 1024], BF16)
        wout = A2([128, 8, 1024], BF16)
        assert A2.off == 50688
        for kc in range(8):
            pg.dma("pool", lambda e, kc=kc: e.dma_start(out=wcx[:, kc, :], in_=w_in[kc * 128:(kc + 1) * 128, 1536:2560]),
                   writes=[("wcx", kc)], stream="ldw")
        for kc in range(8):
            pg.dma("pool", lambda e, kc=kc: e.dma_start(out=wout[:, kc, :], in_=w_out[kc * 128:(kc + 1) * 128, :]),
                   writes=[("wout", kc)], stream="ldw")
        WCX = [("wcx", kc) for kc in range(8)]
        WOUT = [("wout", kc) for kc in range(8)]
        for i in range(3):
            pg.op("pool", lambda e, i=i: e.memset(Vaug[i].rearrange("p a b -> p (a b)"), 1.0), writes=[("Vaug", i)])

        vcount = [0]

        def build_v(kbase, d):
            i = vcount[0] % 3; vcount[0] += 1
            tb = 6
            pb = bank_bf(tb)
            for j in range(4):
                pg.op("pe", lambda e, j=j: e.transpose(out=pb[:, j * 128:(j + 1) * 128],
                                                      in_=vT[:, j, kbase:kbase + 127 * d + 1:d], identity=identb),
                      reads=["identb"], writes=[("ps", tb)], c=0.09)
            pg.op("dve", lambda e: e.tensor_copy(out=Vaug[i][:, :, 0:64], in_=pb[:, 0:512].rearrange("p (a b) -> p a b", a=8)),
                  reads=[("ps", tb)], writes=[("Vaug", i)])
            return i

        ptc = [0]
        grp = [0]
        patterns = [(1, [(128 * t, t) for t in range(16)]),
                    (4, [(512 * u + c, u) for c in range(4) for u in range(4)]),
                    (16, [(r, 0) for r in range(16)])]
        for (d, groups) in patterns:
            prev_v = None
            for gi, (q0, chain) in enumerate(groups):
                kdiag = 2048 + q0
                kprev = kdiag - 128 * d
                if d == 16 or chain == 0 or prev_v is None:
                    vprev = build_v(kprev, d)
                else:
                    vprev = prev_v
                vdiag = build_v(kdiag, d)
                prev_v = vdiag
                halo_prev = kprev < 2048
                if upto == "s1":
                    pg.barrier()
                    pg.emit(final_waits=[pg.dma("sync", lambda e: e.dma_start(out=outd[0:128, 0:8], in_=hb[:, 0:2].bitcast(F32)[:, 0:2].broadcast_to([128, 2]) if False else ss1[:, 0:8]), stream="st")])
                    return nc, dbg_out
                pts = {}
                for kt, kb in ((0, kprev), (1, kdiag)):
                    sbs = [next_bank(), next_bank()]
                    for par in range(2):
                        sb = sbs[par]
                        po = par * 64
                        for hh in range(4):
                            h = hh * 2 + par
                            pg.op("pe", lambda e, sb=sb, hh=hh, h=h, po=po, kb=kb: e.matmul(
                                bank(sb, 128, hh * 128), lhsT=kT[po:po + 64, h // 2, kb:kb + 127 * d + 1:d],
                                rhs=qT[po:po + 64, h // 2, q0:q0 + 127 * d + 1:d], start=True, stop=True),
                                reads=[], writes=[("ps", sb)], c=0.085)
                    for hg in range(2):
                        sb = sbs[hg]
                        pi = ptc[0] % 8; ptc[0] += 1
                        pts[(kt, hg)] = pi
                        ptf = PT[pi].rearrange("p a b -> p (a b)")
                        if kt == 0 and halo_prev:
                            pg.op("act", lambda e, sb=sb, ptf=ptf: e.activation(out=ptf, in_=bank(sb), func=AF.Exp, bias=hb[:, 0:1], scale=0.125),
                                  reads=[("ps", sb), "hb"], writes=[("PT", pi)])
                        else:
                            pg.op("act", lambda e, sb=sb, ptf=ptf: e.activation(out=ptf, in_=bank(sb), func=AF.Exp, scale=0.125),
                                  reads=[("ps", sb)], writes=[("PT", pi)])
                        meng = "dve" if (kt + hg) % 2 == 0 else "pool"
                        pg.op(meng, lambda e, pi=pi, kt=kt: e.tensor_tensor(out=PT[pi], in0=PT[pi],
                                                                          in1=mask2[:, kt, :].unsqueeze(1).broadcast_to([128, 4, 128]), op=ALU.mult),
                              reads=[("PT", pi), "mask2"], writes=[("PT", pi)])
                if upto == "s2":
                    pg.barrier()
                    pg.emit(final_waits=[pg.dma("sync", lambda e: e.dma_start(out=outd[0:128, 0:8], in_=hb[:, 0:2].bitcast(F32)[:, 0:2].broadcast_to([128, 2]) if False else ss1[:, 0:8]), stream="st")])
                    return nc, dbg_out
                for hg in range(2):
                    for hh in range(4):
                        h = hg * 4 + hh
                        for kt, vi in ((0, vprev), (1, vdiag)):
                            pi = pts[(kt, h % 2)]
                            pg.op("pe", lambda e, hg=hg, hh=hh, h=h, pi=pi, vi=vi, kt=kt: e.matmul(
                                psum[:, OB[hg] + hh * 65:OB[hg] + hh * 65 + 65],
                                lhsT=PT[pi][:, h // 2, :], rhs=Vaug[vi][:, h, 0:65], start=(kt == 0), stop=(kt == 1)),
                                reads=[("PT", pi), ("Vaug", vi)], writes=[("ps", 7 if hg == 0 else 0)], c=0.06)
                if upto == "s3":
                    pg.barrier()
                    pg.emit(final_waits=[pg.dma("sync", lambda e: e.dma_start(out=outd[0:128, 0:8], in_=hb[:, 0:2].bitcast(F32)[:, 0:2].broadcast_to([128, 2]) if False else ss1[:, 0:8]), stream="st")])
                    return nc, dbg_out
                osl = grp[0] % 2
                for hg in range(2):
                    ov = psum[:, OB[hg]:OB[hg] + 260].rearrange("p (a b) -> p a b", a=4)
                    okey = ("ps", 7 if hg == 0 else 0)
                    pg.op("act", lambda e, hg=hg, ov=ov: e.copy(out=Obf[osl][:, hg * 4:(hg + 1) * 4, :], in_=ov[:, :, 0:64]),
                          reads=[okey], writes=[("Obf", osl), okey])
                    pg.op("dve", lambda e, hg=hg, ov=ov: e.tensor_copy(out=Zs[:, osl, hg * 4:(hg + 1) * 4], in_=ov[:, :, 64]),
                          reads=[okey], writes=[("Zs", osl), okey])
                if upto == "s4":
                    pg.barrier()
                    pg.emit(final_waits=[pg.dma("sync", lambda e: e.dma_start(out=outd[0:128, 0:8], in_=hb[:, 0:2].bitcast(F32)[:, 0:2].broadcast_to([128, 2]) if False else ss1[:, 0:8]), stream="st")])
                    return nc, dbg_out
                tb = 4 + (trb[0] % 2); trb[0] += 1
                pb = bank_bf(tb)
                of = Obf[osl].rearrange("p a b -> p (a b)")
                for c in range(4):
                    pg.op("pe", lambda e, c=c: e.transpose(out=pb[:, c * 128:(c + 1) * 128], in_=of[:, c * 128:(c + 1) * 128], identity=identb),
                          reads=[("Obf", osl), "identb"], writes=[("ps", tb)], c=0.09)
                pg.op("dve", lambda e: e.tensor_copy(out=Zh[:, osl, 0:8], in_=Zs[:, osl, :]), reads=[("Zs", osl)], writes=[("Zh", osl, 0)], c=0.15)
                pg.op("dve", lambda e: e.tensor_tensor(out=Zh[:, osl, 8:16], in0=Zs[:, osl, :], in1=Zh[:, osl, 0:8], op=ALU.subtract),
                      reads=[("Zs", osl), ("Zh", osl, 0)], writes=[("Zh", osl, 1)], c=0.15)
                for part in range(2):
                    pg.op("pe", lambda e, part=part: e.matmul(psum[0:8, tb * 512 + 256:tb * 512 + 384], lhsT=Zh[:, osl, part * 8:(part + 1) * 8], rhs=identb,
                                                             start=(part == 0), stop=(part == 1)),
                          reads=[("Zh", osl, 0), ("Zh", osl, 1), "identb"], writes=[("ps", tb)], c=0.09)
                accv = ya_acc[:, :, q0:q0 + 127 * d + 1:d]
                zv = ZT_acc[0:8, q0:q0 + 127 * d + 1:d]
                if d == 1:
                    akeys = [("acc", q0 // 512)]
                elif d == 4:
                    akeys = [("acc", q0 // 512)]
                else:
                    akeys = [("acc", i) for i in range(4)]
                psv = pb[:, 0:512].rearrange("p (a b) -> p a b", a=4)
                pz = psum[0:8, tb * 512 + 256:tb * 512 + 384]
                if d == 1:
                    pg.op("dve", lambda e: e.tensor_copy(out=accv, in_=psv), reads=[("ps", tb)], writes=akeys)
                    pg.op("dve", lambda e: e.tensor_copy(out=zv, in_=pz), reads=[("ps", tb)], writes=[k + ("z",) for k in akeys])
                else:
                    pg.op("dve", lambda e: e.tensor_tensor(out=accv, in0=accv, in1=psv, op=ALU.add), reads=[("ps", tb)] + akeys, writes=akeys)
                    pg.op("dve", lambda e: e.tensor_tensor(out=zv, in0=zv, in1=pz, op=ALU.add),
                          reads=[("ps", tb)] + [k + ("z",) for k in akeys], writes=[k + ("z",) for k in akeys])
                grp[0] += 1
                if upto is not None and upto.startswith("g") and grp[0] == int(upto[1:]):
                    finals.append(dump("acc", ya_acc, [128, 4, 2048], F32))
                    finals.append(dump("ZT", ZT_acc[0:8, :], [8, 2048], F32))
                    pg.emit(final_waits=finals)
                    return nc, dbg_out

        for bl in range(4):
            cs_ = slice(bl * 512, (bl + 1) * 512)
            pg.op("dve", lambda e, cs_=cs_: e.reciprocal(out=rZ[0:8, :], in_=ZT_acc[0:8, cs_]),
                  reads=[("acc", bl, "z")], writes=["rZ"])
            pg.op("dve", lambda e: e.tensor_copy(out=rZh[0:8, :], in_=rZ[0:8, :]), reads=["rZ"], writes=["rZhl"])
            pg.op("dve", lambda e: e.tensor_tensor(out=rZl[0:8, :], in0=rZ[0:8, :], in1=rZh[0:8, :], op=ALU.subtract), reads=["rZ", "rZhl"], writes=["rZhl"])
            for c in range(4):
                sb = next_bank()
                pg.op("pe", lambda e, c=c, sb=sb: e.matmul(bank(sb), lhsT=Esel[0:8, c, :], rhs=rZh[0:8, :], start=True, stop=False),
                      reads=["rZhl", "Esel"], writes=[("ps", sb)], c=0.25)
                pg.op("pe", lambda e, c=c, sb=sb: e.matmul(bank(sb), lhsT=Esel[0:8, c, :], rhs=rZl[0:8, :], start=False, stop=True),
                      reads=["rZhl", "Esel"], writes=[("ps", sb)], c=0.25)
                pg.op("dve", lambda e, c=c, sb=sb, cs_=cs_: e.tensor_tensor(out=ya_acc[:, c, cs_], in0=ya_acc[:, c, cs_], in1=bank(sb), op=ALU.mult),
                      reads=[("ps", sb), ("acc", bl)], writes=[("accn", bl, c)])
                sl = c
                pg.op("act", lambda e, c=c, sl=sl, cs_=cs_: e.activation(out=sqA[sl], in_=ya_acc[:, c, cs_], func=AF.Square),
                      reads=[("accn", bl, c)], writes=[("sqA", sl)])
                pg.op("act", lambda e, c=c, cs_=cs_: e.activation(out=ya_T[:, c, cs_], in_=ya_acc[:, c, cs_], func=AF.Copy, scale=ga[:, c:c + 1]),
                      reads=[("accn", bl, c), "gcol"], writes=[("ya_T", bl, c)])
            for ti in range(4):
                for c in range(4):
                    pg.op("pe", lambda e, c=c, ti=ti: e.matmul(psum[:, SSA0 + ti:SSA0 + ti + 1], lhsT=sqA[c][:, ti * 128:(ti + 1) * 128],
                                                              rhs=ones_bf[:, 0:1], start=(c == 0), stop=(c == 3)),
                          reads=[("sqA", c), "ones"], writes=[("ps", 0)], c=0.06)
            pg.op("dve", lambda e, bl=bl: e.tensor_copy(out=ssa[:, bl * 4:(bl + 1) * 4], in_=psum[:, SSA0:SSA0 + 4]),
                  reads=[("ps", 0)], writes=[("ssa", bl)])
            pg.op("act", lambda e, bl=bl: e.activation(out=na[:, bl * 4:(bl + 1) * 4], in_=ssa[:, bl * 4:(bl + 1) * 4], func=AF.Sqrt, bias=EPS, scale=1.0 / 512),
                  reads=[("ssa", bl)], writes=[("na", bl)])
            pg.op("dve", lambda e, bl=bl: e.reciprocal(out=na[:, bl * 4:(bl + 1) * 4], in_=na[:, bl * 4:(bl + 1) * 4]),
                  reads=[("na", bl)], writes=[("na", bl)])

        if "attn" in debug:
            finals.append(dump("yaT", ya_T, [128, 4, 2048], BF16))
            finals.append(dump("na", na, [128, 16], F32))

        if upto == "p2":
            pg.emit(final_waits=finals)
            return nc, dbg_out
        pg.barrier()
        x1, o_ = view(4096, [128, 16, 1024], F32)
        nrot[0] = 2
        OX0 = 3 * 512
        SSC0 = 3 * 512 + 480
        OP0 = 4 * 512
        yxps = bank_bf(3)[:, 640:896]
        A3 = Alloc(20480, 38400)
        hT3_ = [A3([128, 8, 512], BF16) for _ in range(2)]
        sqs3 = A3([128, 1024], BF16)
        xn3 = [A3([128, 1024], BF16) for _ in range(2)]
        c_sb = [A3([128, 512], F32) for _ in range(2)]
        zb = A3([128, 2, 516], F32)
        ytap = [A3([128, 512], F32) for _ in range(2)]
        sqc = [A3([128, 512], BF16) for _ in range(2)]
        ycT_ = [A3([128, 2, 512], BF16) for _ in range(2)]
        xqT = A3([128, 2, 512], BF16)
        PxT = [A3([128, 512], BF16) for _ in range(8)]
        yxf = [A3([128, 4, 64], F32) for _ in range(2)]
        yxb = [A3([128, 256], BF16) for _ in range(2)]
        yxT_ = [A3([128, 2, 512], BF16) for _ in range(2)]
        yv = [A3([128, 1024], F32) for _ in range(2)]
        sqs_, xn_ = sqs3, xn3

        def norm_tile3(src_ap, slot, ss_col, r_col, skey, srckeys):
            pg.op("act", lambda e: e.activation(out=sqs_, in_=src_ap, func=AF.Square, accum_out=ss_col),
                  reads=srckeys, writes=["sqs3", skey], c=1.0)
            pg.op("act", lambda e: e.activation(out=r_col, in_=ss_col, func=AF.Sqrt, bias=EPS, scale=1.0 / D),
                  reads=[skey], writes=[skey + ("r",)], c=0.25)
            pg.op("dve", lambda e: e.reciprocal(out=r_col, in_=r_col), reads=[skey + ("r",)], writes=[skey + ("r",)], c=0.15)
            pg.op("act", lambda e: e.activation(out=xn_[slot], in_=src_ap, func=AF.Copy, scale=r_col),
                  reads=srckeys + [skey + ("r",)], writes=[("xn3", slot)], c=1.0)

        def transpose_to3(dstT, col0, slot, gcols_ap, tbank, dkey):
            pb = bank_bf(tbank)
            for kc in range(8):
                pg.op("pe", lambda e, kc=kc: e.transpose(out=pb[:, kc * 128:(kc + 1) * 128],
                                                        in_=xn_[slot][:, kc * 128:(kc + 1) * 128], identity=identb),
                      reads=[("xn3", slot), "identb"], writes=[("ps", tbank)], c=0.09)
            pg.op("dve", lambda e: e.tensor_tensor(out=dstT[:, :, col0:col0 + 128],
                                                   in0=pb.rearrange("p (a b) -> p a b", a=8),
                                                   in1=gcols_ap.unsqueeze(2).broadcast_to([128, 8, 128]), op=ALU.mult),
                  reads=[("ps", tbank), "gcol"], writes=[dkey], c=1.2)

        def conv_cu(ncols, zoff, hT3, hkey):
            for j in range(2):
                pb = next_bank()
                for kc in range(8):
                    pg.op("pe", lambda e, kc=kc, pb=pb, j=j: e.matmul(bank(pb, ncols), lhsT=wcx[:, kc, 256 + j * 128:256 + (j + 1) * 128],
                                                                    rhs=hT3[:, kc, 0:ncols], start=(kc == 0), stop=(kc == 7)),
                          reads=[hkey] + WCX, writes=[("ps", pb)])
                pg.op("act", lambda e, pb=pb, j=j: e.copy(out=c_sb[j][:, 0:ncols], in_=bank(pb, ncols)), reads=[("ps", pb)], writes=[("c_sb", j)])
            for j in range(2):
                pb = next_bank()
                for kc in range(8):
                    pg.op("pe", lambda e, kc=kc, pb=pb, j=j: e.matmul(bank(pb, ncols), lhsT=wcx[:, kc, 512 + j * 128:512 + (j + 1) * 128],
                                                                    rhs=hT3[:, kc, 0:ncols], start=(kc == 0), stop=(kc == 7)),
                          reads=[hkey] + WCX, writes=[("ps", pb)])
                pg.op("dve", lambda e, pb=pb, j=j: e.tensor_tensor(out=zb[:, j, zoff:zoff + ncols], in0=c_sb[j][:, 0:ncols], in1=bank(pb, ncols), op=ALU.mult),
                      reads=[("ps", pb), ("c_sb", j)], writes=[("z", j)])

        opair = [0]
        def outproj(bl, tiles=(0, 1, 2, 3)):
                if bl < 0:
                    return
                ycT = ycT_[bl % 2]; yxT = yxT_[bl % 2]
                for ti in tiles:
                    t = bl * 4 + ti
                    ysl = t % 2
                    groups3 = [([(ya_T[:, c, t * 128:(t + 1) * 128], c) for c in range(4)], na[:, t:t + 1], [("ya_T", bl, c) for c in range(4)] + [("na", bl)]),
                               ([(ycT[:, j, ti * 128:(ti + 1) * 128], 4 + j) for j in range(2)], ncn[:, t:t + 1], [("ycT", bl % 2, 0), ("ycT", bl % 2, 1), ("ncn", bl)]),
                               ([(yxT[:, j, ti * 128:(ti + 1) * 128], 6 + j) for j in range(2)], nxn[:, t:t + 1], [("yxT", bl % 2, ti), ("nxn", t)])]
                    for gi3, (lhs_list, nscale, rkeys) in enumerate(groups3):
                        pr = opair[0] % 2; opair[0] += 1
                        pbase = OP0 + pr * 1024
                        for nh in range(2):
                            for ii, (lh, wc) in enumerate(lhs_list):
                                pg.op("pe", lambda e, lh=lh, wc=wc, nh=nh, ii=ii, pbase=pbase, n=len(lhs_list): e.matmul(
                                    psum[:, pbase + nh * 512:pbase + (nh + 1) * 512], lhsT=lh, rhs=wout[:, wc, nh * 512:(nh + 1) * 512],
                                    start=(ii == 0), stop=(ii == n - 1)), reads=rkeys + WOUT, writes=[("psP", pr)])
                        pv = psum[:, pbase:pbase + 1024]
                        if gi3 == 0:
                            pg.op("act", lambda e, pv=pv, nscale=nscale, ysl=ysl: e.activation(out=yv[ysl], in_=pv, func=AF.Copy, scale=nscale),
                                  reads=[("psP", pr)] + rkeys, writes=[("yv", ysl)], c=1.0)
                        else:
                            pg.op("dve", lambda e, pv=pv, nscale=nscale, ysl=ysl: e.scalar_tensor_tensor(out=yv[ysl], in0=pv, scalar=nscale, in1=yv[ysl], op0=ALU.mult, op1=ALU.add),
                                  reads=[("psP", pr), ("yv", ysl)] + rkeys, writes=[("yv", ysl)], c=1.2)
                    pg.op("act", lambda e, ysl=ysl, t=t: e.activation(out=sqs_, in_=yv[ysl], func=AF.Square, accum_out=ssy[:, t:t + 1]),
                          reads=[("yv", ysl)], writes=["sqs3", ("ssy", t)], c=1.0)
                    pg.op("act", lambda e, t=t: e.activation(out=ry[:, t:t + 1], in_=ssy[:, t:t + 1], func=AF.Sqrt, bias=EPS, scale=1.0 / D),
                          reads=[("ssy", t)], writes=[("ry", t)])
                    pg.op("dve", lambda e, t=t: e.reciprocal(out=ry[:, t:t + 1], in_=ry[:, t:t + 1]), reads=[("ry", t)], writes=[("ry", t)])
                    pg.op("dve", lambda e, ysl=ysl, t=t: e.scalar_tensor_tensor(out=yv[ysl], in0=yv[ysl], scalar=ry[:, t:t + 1], in1=Gpm, op0=ALU.mult, op1=ALU.mult),
                          reads=[("yv", ysl), ("ry", t), "Gpm"], writes=[("yv", ysl)], c=1.2)
                    pg.op("pool", lambda e, ysl=ysl, t=t: e.tensor_tensor(out=x1[:, t, :], in0=x1[:, t, :], in1=yv[ysl], op=ALU.add),
                          reads=[("yv", ysl), ("x1", t)], writes=[("x1", t)], c=2.4)


        def front3(bl):
            for ti in range(4):
                t = bl * 4 + ti
                pg.dma("sync", lambda e, t=t: e.dma_start(out=x1[:, t, :], in_=xo[t * 128:(t + 1) * 128, :]), writes=[("x1", t)], stream="ldx")
                sl = t % 2
                norm_tile3(x1[:, t, :], sl, ss3[:, t:t + 1], r3[:, t:t + 1], ("s3", t), [("x1", t)])
                transpose_to3(hT3_[bl % 2], ti * 128, sl, gpre, 0, ("hT3", bl % 2))

        def prologue3():
            pg.dma("sync", lambda e: e.dma_start(out=yv[0], in_=xh[15 * 128:16 * 128, :]), writes=[("yv", 0)], stream="ldx")
            norm_tile3(yv[0], 0, ss3[:, 16:17], r3[:, 16:17], ("s3", 16), [("yv", 0)])
            transpose_to3(hT3_[1], 0, 0, gpre, 0, ("hT3", 1))
            conv_cu(128, 4, hT3_[1], ("hT3", 1))
            for j in range(2):
                pg.op("pool", lambda e, j=j: e.tensor_copy(out=zb[:, j, 0:2], in_=zb[:, j, 130:132]), reads=[("z", j)], writes=[("z", j)])


        front3(0)
        prologue3()
        pg.dma("sync", lambda e: e.dma_start(out=Gpm, in_=gpmd.partition_broadcast(128)), writes=["Gpm"], stream="ld")
        pg.dma("sync", lambda e: e.dma_start(out=Gpl, in_=gpld.partition_broadcast(128)), writes=["Gpl"], stream="ld")
        for bl in range(4):
            ycT = ycT_[bl % 2]; yxT = yxT_[bl % 2]
            hT3 = hT3_[bl % 2]; hkey = ("hT3", bl % 2)
            outproj(bl - 1, (0,))
            conv_cu(512, 2, hT3, hkey)
            if bl + 1 < 4:
                front3(bl + 1)
            for j in range(2):
                w0 = convw[:, j * 3 + 0:j * 3 + 1]; w1 = convw[:, j * 3 + 1:j * 3 + 2]; w2 = convw[:, j * 3 + 2:j * 3 + 3]
                pg.op("dve", lambda e, j=j, w2=w2: e.tensor_scalar(out=ytap[j], in0=zb[:, j, 2:514], scalar1=w2, scalar2=None, op0=ALU.mult),
                      reads=[("z", j), "gcol"], writes=[("ytap", j)])
                pg.op("dve", lambda e, j=j, w1=w1: e.scalar_tensor_tensor(out=ytap[j], in0=zb[:, j, 1:513], scalar=w1, in1=ytap[j], op0=ALU.mult, op1=ALU.add),
                      reads=[("z", j), ("ytap", j)], writes=[("ytap", j)])
                pg.op("dve", lambda e, j=j, w0=w0: e.scalar_tensor_tensor(out=ytap[j], in0=zb[:, j, 0:512], scalar=w0, in1=ytap[j], op0=ALU.mult, op1=ALU.add),
                      reads=[("z", j), ("ytap", j)], writes=[("ytap", j)])
                pg.op("pool", lambda e, j=j: e.tensor_copy(out=zb[:, j, 0:2], in_=zb[:, j, 512:514]), reads=[("z", j)], writes=[("z", j)])
            for j in range(2):
                pb = next_bank()
                for kc in range(8):
                    pg.op("pe", lambda e, kc=kc, pb=pb, j=j: e.matmul(bank(pb), lhsT=wcx[:, kc, j * 128:(j + 1) * 128], rhs=hT3[:, kc, :],
                                                                    start=(kc == 0), stop=(kc == 7)), reads=[hkey] + WCX, writes=[("ps", pb)])
                pg.op("dve", lambda e, pb=pb, j=j: e.tensor_tensor(out=ytap[j], in0=ytap[j], in1=bank(pb), op=ALU.mult),
                      reads=[("ps", pb), ("ytap", j)], writes=[("ytap", j)])
                pg.op("act", lambda e, j=j: e.activation(out=sqc[j], in_=ytap[j], func=AF.Square), reads=[("ytap", j)], writes=[("sqc", j)])
                pg.op("act", lambda e, j=j: e.activation(out=ycT[:, j, :], in_=ytap[j], func=AF.Copy, scale=gc[:, j:j + 1]),
                      reads=[("ytap", j), "gcol"], writes=[("ycT", bl % 2, j)])
            for ti in range(4):
                for j in range(2):
                    pg.op("pe", lambda e, j=j, ti=ti: e.matmul(psum[:, SSC0 + ti:SSC0 + ti + 1], lhsT=sqc[j][:, ti * 128:(ti + 1) * 128],
                                                              rhs=ones_bf[:, 0:1], start=(j == 0), stop=(j == 1)),
                          reads=[("sqc", j), "ones"], writes=[("ps", 3)], c=0.06)
            pg.op("dve", lambda e, bl=bl: e.tensor_copy(out=ssc[:, bl * 4:(bl + 1) * 4], in_=psum[:, SSC0:SSC0 + 4]), reads=[("ps", 3)], writes=[("ssc", bl)])
            pg.op("act", lambda e, bl=bl: e.activation(out=ncn[:, bl * 4:(bl + 1) * 4], in_=ssc[:, bl * 4:(bl + 1) * 4], func=AF.Sqrt, bias=EPS, scale=1.0 / 256),
                  reads=[("ssc", bl)], writes=[("ncn", bl)])
            pg.op("dve", lambda e, bl=bl: e.reciprocal(out=ncn[:, bl * 4:(bl + 1) * 4], in_=ncn[:, bl * 4:(bl + 1) * 4]), reads=[("ncn", bl)], writes=[("ncn", bl)])
            outproj(bl - 1, (1,))
            for j in range(2):
                pb = next_bank()
                for kc in range(8):
                    pg.op("pe", lambda e, kc=kc, pb=pb, j=j: e.matmul(bank(pb), lhsT=wcx[:, kc, 768 + j * 128:768 + (j + 1) * 128], rhs=hT3[:, kc, :],
                                                                    start=(kc == 0), stop=(kc == 7)), reads=[hkey] + WCX, writes=[("ps", pb)])
                pg.op("act", lambda e, pb=pb, j=j: e.copy(out=xqT[:, j, :], in_=bank(pb)), reads=[("ps", pb)], writes=[("xqT", j)])
            for hx in range(4):
                po = (hx % 2) * 64
                for m in range(2):
                    pb = next_bank()
                    if m == 0:
                        pg.op("pe", lambda e, pb=pb: e.matmul(bank(pb, 1, 0), lhsT=identb, rhs=ones_bf[:, 0:1], start=True, stop=True),
                              reads=["identb", "ones"], writes=[("ps", pb), "rowgrp"], c=0.06)
                    pg.op("pe", lambda e, pb=pb, hx=hx, po=po, m=m: e.matmul(bank(pb), lhsT=kmT[po:po + 64, hx // 2, m * 128:(m + 1) * 128],
                                                                           rhs=xqT[po:po + 64, hx // 2, :], start=True, stop=True),
                          reads=["kmT", ("xqT", hx // 2)], writes=[("ps", pb), "rowgrp"])
                    pg.op("act", lambda e, pb=pb, hx=hx, m=m: e.activation(out=PxT[hx * 2 + m], in_=bank(pb), func=AF.Exp, scale=0.125),
                          reads=[("ps", pb)], writes=[("PxT", hx * 2 + m)])
            outproj(bl - 1, (2,))
            for ti in range(4):
                t = bl * 4 + ti
                for hx in range(4):
                    for m in range(2):
                        pg.op("pe", lambda e, hx=hx, m=m, ti=ti: e.matmul(psum[:, OX0 + hx * 65:OX0 + hx * 65 + 65],
                                                                         lhsT=PxT[hx * 2 + m][:, ti * 128:(ti + 1) * 128], rhs=vm_aug[:, m, hx, 0:65],
                                                                         start=(m == 0), stop=(m == 1)),
                              reads=[("PxT", hx * 2 + m), "vm_aug"], writes=[("ps", 3)], c=0.06)
                ov = psum[:, OX0:OX0 + 260].rearrange("p (a b) -> p a b", a=4)
                ys = t % 2
                pg.op("dve", lambda e, ov=ov: e.reciprocal(out=rzx[:, 0:4], in_=ov[:, :, 64]), reads=[("ps", 3)], writes=["rzx"])
                pg.op("dve", lambda e, ov=ov, ys=ys: e.tensor_tensor(out=yxf[ys], in0=ov[:, :, 0:64], in1=rzx[:, 0:4].unsqueeze(2).broadcast_to([128, 4, 64]), op=ALU.mult),
                      reads=[("ps", 3), "rzx"], writes=[("yxf", ys)])
                yflat = yxf[ys].rearrange("p a b -> p (a b)")
                pg.op("act", lambda e, ys=ys, yflat=yflat, t=t: e.activation(out=yxb[ys], in_=yflat, func=AF.Square, accum_out=ssx[:, t:t + 1]),
                      reads=[("yxf", ys)], writes=[("yxb", ys), ("ssx", t)])
                pg.op("act", lambda e, t=t: e.activation(out=nxn[:, t:t + 1], in_=ssx[:, t:t + 1], func=AF.Sqrt, bias=EPS, scale=1.0 / 256),
                      reads=[("ssx", t)], writes=[("nxn", t)])
                pg.op("dve", lambda e, t=t: e.reciprocal(out=nxn[:, t:t + 1], in_=nxn[:, t:t + 1]), reads=[("nxn", t)], writes=[("nxn", t)])
                pg.op("act", lambda e, ys=ys, yflat=yflat: e.copy(out=yxb[ys], in_=yflat), reads=[("yxf", ys), ("yxb", ys)], writes=[("yxb", ys)])
                for jj in range(2):
                    pg.op("pe", lambda e, jj=jj, ys=ys: e.transpose(out=yxps[:, jj * 128:(jj + 1) * 128], in_=yxb[ys][:, jj * 128:(jj + 1) * 128], identity=identb),
                          reads=[("yxb", ys), "identb"], writes=[("ps", 3)], c=0.09)
                pg.op("dve", lambda e, ti=ti: e.tensor_tensor(out=yxT[:, :, ti * 128:(ti + 1) * 128], in0=yxps.rearrange("p (a b) -> p a b", a=2),
                                                             in1=gx.unsqueeze(2).broadcast_to([128, 2, 128]), op=ALU.mult),
                      reads=[("ps", 3), "gcol"], writes=[("yxT", bl % 2, ti)])
            outproj(bl - 1, (3,))
        outproj(3)
        if "x1" in debug:
            finals.append(dump("x1", x1, [128, 16, 1024], F32))

        if upto == "p3":
            pg.emit(final_waits=finals)
            return nc, dbg_out
        pg.barrier(reorder=False)
        nrot[0] = 3
        A4 = Alloc(20480, ARENA_WORDS)
        h2T = A4([128, 8, 1024], BF16)
        acc = A4([128, 8, 1024], F32)
        sqs4 = A4([128, 1024], BF16)
        xn4 = [A4([128, 1024], BF16) for _ in range(2)]
        wu = [A4([128, 8, 512], BF16) for _ in range(2)]
        wd = [A4([128, 4, 1024], BF16) for _ in range(2)]
        fT = [A4([128, 4, 1024], BF16) for _ in range(2)]
        tmpR = [A4([128, 512], BF16) for _ in range(2)]
        yv4 = [A4([128, 1024], F32) for _ in range(2)]
        sqs_, xn_ = sqs4, xn4
        wdv = w_down.rearrange("(c s p) d -> c p s d", s=4, p=128)
        upc = [0]
        last_store = None
        def front4a(tb_, ti):
            t = tb_ * 8 + ti
            norm_tile3(x1[:, t, :], t % 2, ss4[:, t:t + 1], r4[:, t:t + 1], ("s4", t), [("x1", t)])

        def front4b(tb_, ti):
            t = tb_ * 8 + ti
            transpose_to3(h2T, ti * 128, t % 2, gpremlp, 0, ("h2T", ti))

        def front4(tb_, ti):
            front4a(tb_, ti); front4b(tb_, ti)

        def load_w4(fc):
            ws = fc % 2
            for kc in range(8):
                pg.dma("pool", lambda e, kc=kc, ws=ws, fc=fc: e.dma_start(out=wu[ws][:, kc, :], in_=w_up[kc * 128:(kc + 1) * 128, fc * 512:(fc + 1) * 512]),
                       writes=[("wu", ws, kc)], stream="ldw")
            pg.dma("pool", lambda e, ws=ws, fc=fc: e.dma_start(out=wd[ws], in_=wdv[fc]), writes=[("wd", ws)], stream="ldw")

        load_w4(0)
        for ti in range(4):
            front4(0, ti)
        for tb_ in range(2):
            for fc in range(8):
                ws = fc % 2
                if fc > 0:
                    load_w4(fc)
                WU = [("wu", ws, kc) for kc in range(8)]
                first = (tb_ == 0 and fc == 0)
                sth = [(s_, th_) for th_ in range(2) for s_ in range(4)] if first else [(s_, th_) for s_ in range(4) for th_ in range(2)]
                for (s, th) in sth:
                    if True:
                        H2 = [("h2T", ti) for ti in range(th * 4, th * 4 + 4)]
                        if first and th == 0 and s == 0:
                            front4a(0, 4)
                        pb = next_bank()
                        for kc in range(8):
                            pg.op("pe", lambda e, kc=kc, pb=pb, s=s, th=th, ws=ws: e.matmul(bank(pb), lhsT=wu[ws][:, kc, s * 128:(s + 1) * 128],
                                                                                          rhs=h2T[:, kc, th * 512:(th + 1) * 512], start=(kc == 0), stop=(kc == 7)),
                                  reads=WU + H2, writes=[("ps", pb)])
                        rs = upc[0] % 2; upc[0] += 1
                        pg.op("act", lambda e, pb=pb, rs=rs: e.activation(out=tmpR[rs], in_=bank(pb), func=AF.Relu), reads=[("ps", pb)], writes=[("tmpR", rs)])
                        pg.op("dve", lambda e, rs=rs, s=s, th=th, ws=ws: e.tensor_tensor(out=fT[ws][:, s, th * 512:(th + 1) * 512], in0=tmpR[rs], in1=tmpR[rs], op=ALU.mult),
                              reads=[("tmpR", rs)], writes=[("fT", ws, s, th)])
                        if first and th == 0:
                            front4b(0, 4 + s)
                            if s < 3:
                                front4a(0, 5 + s)
                nxt = (fc == 7 and tb_ == 0)
                if nxt:
                    load_w4(0)
                    front4a(1, 0)
                for ti in range(8):
                    t = tb_ * 8 + ti
                    if nxt and ti >= 1:
                        front4b(1, ti - 1)
                        front4a(1, ti)
                    pr = opair[0] % 2; opair[0] += 1
                    pbase = OP0 + pr * 1024
                    for nh in range(2):
                        for s in range(4):
                            pg.op("pe", lambda e, s=s, nh=nh, ti=ti, ws=ws, pbase=pbase: e.matmul(
                                psum[:, pbase + nh * 512:pbase + (nh + 1) * 512], lhsT=fT[ws][:, s, ti * 128:(ti + 1) * 128],
                                rhs=wd[ws][:, s, nh * 512:(nh + 1) * 512], start=(s == 0), stop=(s == 3)),
                                reads=[("fT", ws, s, ti // 4), ("wd", ws)], writes=[("psP", pr)])
                    pv = psum[:, pbase:pbase + 1024]
                    if fc == 0:
                        pg.op("act", lambda e, pv=pv, ti=ti: e.copy(out=acc[:, ti, :], in_=pv), reads=[("psP", pr)], writes=[("acc4", ti)])
                    elif fc < 7:
                        pg.op("dve", lambda e, pv=pv, ti=ti: e.tensor_tensor(out=acc[:, ti, :], in0=acc[:, ti, :], in1=pv, op=ALU.add),
                              reads=[("psP", pr), ("acc4", ti)], writes=[("acc4", ti)])
                    else:
                        ysl = t % 2
                        pg.op("dve", lambda e, pv=pv, ti=ti, ysl=ysl: e.tensor_tensor(out=yv4[ysl], in0=acc[:, ti, :], in1=pv, op=ALU.add),
                              reads=[("psP", pr), ("acc4", ti)], writes=[("yv4", ysl)])
                        pg.op("act", lambda e, ysl=ysl, t=t: e.activation(out=sqs_, in_=yv4[ysl], func=AF.Square, accum_out=ssf[:, t:t + 1]),
                              reads=[("yv4", ysl)], writes=["sqs3", ("ssf", t)])
                        pg.op("act", lambda e, t=t: e.activation(out=rf[:, t:t + 1], in_=ssf[:, t:t + 1], func=AF.Sqrt, bias=EPS, scale=1.0 / D),
                              reads=[("ssf", t)], writes=[("rf", t)])
                        pg.op("dve", lambda e, t=t: e.reciprocal(out=rf[:, t:t + 1], in_=rf[:, t:t + 1]), reads=[("rf", t)], writes=[("rf", t)])
                        pg.op("dve", lambda e, ysl=ysl, t=t: e.scalar_tensor_tensor(out=yv4[ysl], in0=yv4[ysl], scalar=rf[:, t:t + 1], in1=Gpl, op0=ALU.mult, op1=ALU.mult),
                              reads=[("yv4", ysl), ("rf", t), "Gpl"], writes=[("yv4", ysl)])
                        pg.op("pool", lambda e, ysl=ysl, t=t: e.tensor_tensor(out=yv4[ysl], in0=yv4[ysl], in1=x1[:, t, :], op=ALU.add),
                              reads=[("yv4", ysl), ("x1", t)], writes=[("yv4", ysl)])
                        last_store = pg.dma("sync", lambda e, ysl=ysl, t=t: e.dma_start(out=outd[t * 128:(t + 1) * 128, :], in_=yv4[ysl]),
                                            reads=[("yv4", ysl)], stream="st")
                if nxt:
                    front4b(1, 7)
        finals.append(last_store)
        pg.emit(final_waits=finals)
    return nc, dbg_out


_PROG_CACHE = {}


def _host_constants():
    ident = np.eye(128, dtype=np.float32)
    j = np.arange(128)[:, None]; i = np.arange(128)[None, :]
    mask2 = np.concatenate([(j >= i).astype(np.float32), (j <= i).astype(np.float32)], axis=1)
    half = 32
    invf = (np.float32(10000.0) ** (-(np.arange(half, dtype=np.float32) * np.float32(2.0) / np.float32(64)))).astype(np.float32)
    invf = np.tile(invf[None, :], (128, 1)).astype(np.float32)
    E = np.zeros((8, 4, 128), np.float32)
    for h in range(8):
        E[h, h // 2, (h % 2) * 64:(h % 2) * 64 + 64] = 1.0
    return ident, mask2, invf, E.reshape(8, 512)


def make_in_maps(x, mem, positions, g_pre_mix, g_mem, w_in, w_mem_kv, conv_w, g_attn_out, g_conv_out,
                 g_xattn_out, w_out, g_post_mix, g_pre_mlp, w_up, w_down, g_post_mlp):
    f = lambda a: np.ascontiguousarray(np.asarray(a))
    x = f(x); mem = f(mem); positions = f(positions)
    ident, mask2, invf, E = _host_constants()
    gcol = np.zeros((128, 40), np.float32)
    gcol[:, 0:8] = f(g_pre_mix)[0].reshape(8, 128).T
    gcol[:, 8:16] = f(g_mem)[0].reshape(8, 128).T
    gcol[:, 16:24] = f(g_pre_mlp)[0].reshape(8, 128).T
    gcol[:, 24:28] = f(g_attn_out)[0].reshape(4, 128).T
    gcol[:, 28:30] = f(g_conv_out)[0].reshape(2, 128).T
    gcol[:, 30:32] = f(g_xattn_out)[0].reshape(2, 128).T
    cw = f(conv_w)[0]
    for j in range(2):
        for tap in range(3):
            gcol[:, 32 + j * 3 + tap] = cw[tap, j * 128:(j + 1) * 128]
    shared = dict(w_in=f(w_in)[0], w_mem_kv=f(w_mem_kv)[0], w_out=f(w_out)[0], w_up=f(w_up)[0], w_down=f(w_down)[0],
                  gcol=gcol, g_post_mix=f(g_post_mix)[0], g_post_mlp=f(g_post_mlp)[0],
                  ident=ident, mask2=mask2, invf=invf, Esel=E)
    in_maps = []
    for c in range(8):
        b, half = c // 2, c % 2
        xo = x[b, half * NOWN:(half + 1) * NOWN]
        if half == 1:
            xh = x[b, 0:NOWN]
            pos = positions[b, 0:SEQ]
            hbv = 0.0
        else:
            xh = np.zeros((NOWN, D), np.float32)
            pos = np.concatenate([np.zeros(NOWN, np.int32), positions[b, 0:NOWN]])
            hbv = -30000.0
        posT = np.ascontiguousarray(pos.reshape(32, 128).T.astype(np.int32))
        m = dict(shared)
        m.update(xo=np.ascontiguousarray(xo), xh=np.ascontiguousarray(xh), mem=np.ascontiguousarray(mem[b]),
                 posT=posT, hb=np.full((128, 1), hbv, np.float32))
        in_maps.append(m)
    return in_maps


def kernel(**inputs):
    in_maps = make_in_maps(**inputs)
    if "nc" not in _PROG_CACHE:
        _PROG_CACHE["nc"] = build_program()[0]
    nc = _PROG_CACHE["nc"]
    res = run_bass_kernel_spmd(nc, in_maps, core_ids=list(range(8)))
    out = np.zeros((NB, SEQ, D), np.float32)
    for c in range(8):
        b, half = c // 2, c % 2
        out[b, half * NOWN:(half + 1) * NOWN] = res.results[c]["out"]
    return out
```
